# Optimizing a Trainium2 kernel written in Bass

```python
import math
import jax, jax.numpy as jnp
from jax import lax
import numpy as np

D_MODEL = 2048
BATCH = 1
SEQ = 8192
DEPTH = 4
DEC_BATCH = 4
DEC_SEQ = 4096
PAST_LEN = 128

HEAD_DIM = 128
N_SGU_GROUPS = 4
SGU_WIDTH = N_SGU_GROUPS * HEAD_DIM
SGU_CHUNK = 128
N_DIL_HEADS = 8
DIL_WIDTH = N_DIL_HEADS * HEAD_DIM
DIL_PAIRS = ((128, 1), (512, 4), (2048, 16))
N_MEM_HEADS = 4
MEM_WIDTH = N_MEM_HEADS * HEAD_DIM
N_MEM = 256

MIX_WIDTH = SGU_WIDTH + DIL_WIDTH + MEM_WIDTH
IN_WIDTH = 2 * SGU_WIDTH + 3 * DIL_WIDTH + MEM_WIDTH
ROPE_THETA = 500000.0
ROPE_DIM = HEAD_DIM // 4
FFN_HIDDEN = -(-8 * D_MODEL // (3 * 256)) * 256
EPS = 1e-6
NEG_INF = -1e30

kernel_name = 'hybrid_sgu_dilated_memory_encoder'


def rmsnorm(x, g):
    xf = x.astype(jnp.float32)
    y = xf * lax.rsqrt(jnp.mean(xf * xf, axis=-1, keepdims=True) + EPS)
    return (y * g.astype(jnp.float32)).astype(x.dtype)


def rope_partial(t, positions):
    inv = ROPE_THETA ** (-jnp.arange(0, ROPE_DIM, 2, dtype=jnp.float32) / ROPE_DIM)
    ang = positions.astype(jnp.float32)[:, None] * inv[None, :]
    cos = jnp.cos(ang)[None, :, None, :]
    sin = jnp.sin(ang)[None, :, None, :]
    tf = t.astype(jnp.float32)
    x1 = tf[..., :ROPE_DIM // 2]
    x2 = tf[..., ROPE_DIM // 2:ROPE_DIM]
    out = jnp.concatenate([x1 * cos - x2 * sin, x2 * cos + x1 * sin, tf[..., ROPE_DIM:]], axis=-1)
    return out.astype(t.dtype)


def dilated_branch(q, k, v, window, dilation):
    B, S, H, Dh = q.shape
    half = window // (2 * dilation)
    L = S // dilation
    nb = -(-L // half)
    Lp = nb * half

    def residues(t):
        return t.reshape(B, L, dilation, H, Dh).transpose(0, 2, 1, 3, 4)

    qr = jnp.pad(residues(q), ((0, 0), (0, 0), (0, Lp - L), (0, 0), (0, 0)))
    qb = qr.reshape(B, dilation, nb, half, H, Dh)

    def key_blocks(t):
        tp = jnp.pad(residues(t), ((0, 0), (0, 0), (half, Lp - L + half), (0, 0), (0, 0)))
        parts = [tp[:, :, j * half:j * half + Lp].reshape(B, dilation, nb, half, H, Dh) for j in range(3)]
        return jnp.concatenate(parts, axis=3)

    kb = key_blocks(k)
    vb = key_blocks(v)
    s = jnp.einsum('brnqhd,brnkhd->brnhqk', qb, kb).astype(jnp.float32) * (Dh ** -0.5)
    qpos = jnp.arange(nb)[:, None] * half + jnp.arange(half)[None, :]
    kpos = jnp.arange(nb)[:, None] * half - half + jnp.arange(3 * half)[None, :]
    kp = kpos[:, None, :]
    valid = (jnp.abs(kp - qpos[:, :, None]) <= half) & (kp >= 0) & (kp < L)
    s = jnp.where(valid[None, None, :, None, :, :], s, NEG_INF)
    m = jnp.max(s, axis=-1)
    p = jnp.exp(s - m[..., None])
    den = jnp.sum(p, axis=-1)
    num = jnp.einsum('brnhqk,brnkhd->brnqhd', p, vb.astype(jnp.float32))
    num = num.reshape(B, dilation, Lp, H, Dh)[:, :, :L].transpose(0, 2, 1, 3, 4).reshape(B, S, H, Dh)

    def stat(a):
        a = a.transpose(0, 1, 2, 4, 3).reshape(B, dilation, Lp, H)[:, :, :L]
        return a.transpose(0, 2, 1, 3).reshape(B, S, H)

    return num, stat(den), stat(m)


def dilated_attention(q, k, v):
    branches = [dilated_branch(q, k, v, w, d) for (w, d) in DIL_PAIRS]
    m_all = branches[0][2]
    for br in branches[1:]:
        m_all = jnp.maximum(m_all, br[2])
    numer = 0.0
    denom = 0.0
    for num, den, m in branches:
        scale = jnp.exp(m - m_all)
        numer = numer + scale[..., None] * num
        denom = denom + scale * den
    return numer / denom[..., None]


def layer(x, mem, pos, g_mix_norm, w_in, g_sgu, w_spatial, b_spatial, g_mem_norm, w_mem_kv,
          g_group_out, w_out, g_ffn_norm, w_gate_up, w_down):
    B, S, _ = x.shape
    h = rmsnorm(x, g_mix_norm)
    z = h @ w_in
    o_q = 2 * SGU_WIDTH
    o_k = o_q + DIL_WIDTH
    o_v = o_k + DIL_WIDTH
    o_c = o_v + DIL_WIDTH

    za = jax.nn.gelu(z[..., :o_q])
    u = za[..., :SGU_WIDTH]
    vv = rmsnorm(za[..., SGU_WIDTH:].reshape(B, S, N_SGU_GROUPS, HEAD_DIM), g_sgu)
    vc = vv.reshape(B, S // SGU_CHUNK, SGU_CHUNK, N_SGU_GROUPS, HEAD_DIM)
    vs = jnp.einsum('gtp,bnpgc->bntgc', w_spatial, vc) + b_spatial.T[None, None, :, :, None]
    a_out = u * vs.reshape(B, S, SGU_WIDTH)

    qb = rope_partial(z[..., o_q:o_k].reshape(B, S, N_DIL_HEADS, HEAD_DIM), pos)
    kb = rope_partial(z[..., o_k:o_v].reshape(B, S, N_DIL_HEADS, HEAD_DIM), pos)
    vb = z[..., o_v:o_c].reshape(B, S, N_DIL_HEADS, HEAD_DIM)
    b_out = dilated_attention(qb, kb, vb).reshape(B, S, DIL_WIDTH).astype(x.dtype)

    qc = z[..., o_c:].reshape(B, S, N_MEM_HEADS, HEAD_DIM)
    kv = (rmsnorm(mem, g_mem_norm) @ w_mem_kv).reshape(B, N_MEM, 2, N_MEM_HEADS, HEAD_DIM)
    sc = jnp.einsum('bshd,bmhd->bhsm', qc, kv[:, :, 0]).astype(jnp.float32) * (HEAD_DIM ** -0.5)
    pc = jax.nn.softmax(sc, axis=-1)
    c_out = jnp.einsum('bhsm,bmhd->bshd', pc, kv[:, :, 1].astype(jnp.float32))
    c_out = c_out.reshape(B, S, MEM_WIDTH).astype(x.dtype)

    g1 = SGU_WIDTH
    g2 = SGU_WIDTH + DIL_WIDTH
    mix = jnp.concatenate([rmsnorm(a_out, g_group_out[:g1]),
                           rmsnorm(b_out, g_group_out[g1:g2]),
                           rmsnorm(c_out, g_group_out[g2:])], axis=-1)
    x = x + mix @ w_out

    gu = rmsnorm(x, g_ffn_norm) @ w_gate_up
    gate = gu[..., :FFN_HIDDEN]
    up = gu[..., FFN_HIDDEN:]
    x = x + (jax.nn.silu(gate) * up) @ w_down
    return x


def trunk(x, mem, g_mix_norm, w_in, g_sgu, w_spatial, b_spatial, g_mem_norm, w_mem_kv,
          g_group_out, w_out, g_ffn_norm, w_gate_up, w_down, g_final):
    pos = jnp.arange(x.shape[1], dtype=jnp.int32)
    for l in range(DEPTH):
        x = layer(x, mem, pos, g_mix_norm[l], w_in[l], g_sgu[l], w_spatial[l], b_spatial[l],
                  g_mem_norm[l], w_mem_kv[l], g_group_out[l], w_out[l], g_ffn_norm[l],
                  w_gate_up[l], w_down[l])
    return rmsnorm(x, g_final)


def setup_inputs(seed: int = 0) -> dict:
    key = jax.random.key(seed)
    ks = jax.random.split(key, 20)
    f32 = jnp.float32

    def nrm(k, shape, scale):
        return jax.random.normal(k, shape, f32) * scale

    def gain(k, shape):
        return 1.0 + 0.02 * jax.random.normal(k, shape, f32)

    return {
        'x_prompt': nrm(ks[0], (BATCH, SEQ, D_MODEL), 1.0),
        'x_sample': nrm(ks[1], (DEC_BATCH, DEC_SEQ, D_MODEL), 1.0),
        'mem_prompt': nrm(ks[2], (BATCH, N_MEM, D_MODEL), 1.0),
        'mem_sample': nrm(ks[3], (DEC_BATCH, N_MEM, D_MODEL), 1.0),
        'g_mix_norm': gain(ks[4], (DEPTH, D_MODEL)),
        'w_in': nrm(ks[5], (DEPTH, D_MODEL, IN_WIDTH), D_MODEL ** -0.5),
        'g_sgu': gain(ks[6], (DEPTH, N_SGU_GROUPS, HEAD_DIM)),
        'w_spatial': nrm(ks[7], (DEPTH, N_SGU_GROUPS, SGU_CHUNK, SGU_CHUNK), SGU_CHUNK ** -0.5),
        'b_spatial': 1.0 + 0.01 * jax.random.normal(ks[8], (DEPTH, N_SGU_GROUPS, SGU_CHUNK), f32),
        'g_mem_norm': gain(ks[9], (DEPTH, D_MODEL)),
        'w_mem_kv': nrm(ks[10], (DEPTH, D_MODEL, 2 * MEM_WIDTH), D_MODEL ** -0.5),
        'g_group_out': gain(ks[11], (DEPTH, MIX_WIDTH)),
        'w_out': nrm(ks[12], (DEPTH, MIX_WIDTH, D_MODEL), MIX_WIDTH ** -0.5),
        'g_ffn_norm': gain(ks[13], (DEPTH, D_MODEL)),
        'w_gate_up': nrm(ks[14], (DEPTH, D_MODEL, 2 * FFN_HIDDEN), D_MODEL ** -0.5),
        'w_down': nrm(ks[15], (DEPTH, FFN_HIDDEN, D_MODEL), FFN_HIDDEN ** -0.5),
        'g_final': gain(ks[16], (D_MODEL,)),
    }


def reference(x_prompt, x_sample, mem_prompt, mem_sample, g_mix_norm, w_in, g_sgu, w_spatial,
              b_spatial, g_mem_norm, w_mem_kv, g_group_out, w_out, g_ffn_norm, w_gate_up,
              w_down, g_final):
    y_prompt = trunk(x_prompt, mem_prompt, g_mix_norm, w_in, g_sgu, w_spatial, b_spatial,
                     g_mem_norm, w_mem_kv, g_group_out, w_out, g_ffn_norm, w_gate_up, w_down, g_final)
    y_sample = trunk(x_sample, mem_sample, g_mix_norm, w_in, g_sgu, w_spatial, b_spatial,
                     g_mem_norm, w_mem_kv, g_group_out, w_out, g_ffn_norm, w_gate_up, w_down, g_final)
    return (y_prompt, y_sample)
```

```python
import numpy as np, math
import concourse.bass as bass
import concourse.mybir as mybir
from concourse.bass_utils import run_bass_kernel_spmd

F32 = mybir.dt.float32
BF16 = mybir.dt.bfloat16
ALU = mybir.AluOpType
AF = mybir.ActivationFunctionType

CENG = ('pe', 'act', 'dve', 'pool')
ENGS = ('pe', 'act', 'dve', 'pool', 'sp')
NSEM = 24


class Prog:
    def __init__(self, nc):
        self.nc = nc
        self.ops = {e: [] for e in ENGS}
        self.nins = {e: 0 for e in ENGS}
        self.insref = {e: [] for e in ENGS}
        self.clock = {e: [] for e in ENGS}
        self.known = {e: {c: -1 for c in CENG} for e in ENGS}
        self.known_dma = {e: set() for e in ENGS}
        self.dmas = []
        self.dma_count = {e: 0 for e in ENGS}
        self.dma_ids = {e: [] for e in ENGS}
        self.dma_ids['cc'] = []
        self.segs = {}

    @staticmethod
    def _region(ap):
        sp = str(ap.space)
        if 'SB' not in sp.upper() and 'PSUM' not in sp.upper():
            return None
        pat = ap.ap
        P = pat[0][0]
        off = ap.offset
        lo = off % P if P > 0 else off
        hi = lo + 1
        for st, cnt in pat[1:]:
            hi += abs(st) * (cnt - 1)
        es = 4 if ap.dtype == F32 else 2
        return (ap.tensor.name, lo * es, hi * es)

    def _access(self, reg, is_write, ev, deps):
        name, lo, hi = reg
        segs = self.segs.setdefault(name, [])
        out = []
        covered = []
        for s in segs:
            slo, shi, w, r = s
            if shi <= lo or slo >= hi:
                out.append(s)
                continue
            deps.extend(w)
            if is_write:
                deps.extend(r)
            if slo < lo:
                out.append((slo, lo, w, r))
            if shi > hi:
                out.append((hi, shi, w, r))
            ilo, ihi = max(slo, lo), min(shi, hi)
            if not is_write:
                nr = [x for x in r if not (x[0] == 'e' and ev[0] == 'e' and x[1] == ev[1])]
                nr.append(ev)
                out.append((ilo, ihi, w, nr))
                covered.append((ilo, ihi))
        if is_write:
            out.append((lo, hi, [ev], []))
        else:
            covered.sort()
            cur = lo
            for a, b in covered:
                if a > cur:
                    out.append((cur, a, [], [ev]))
                cur = max(cur, b)
            if cur < hi:
                out.append((cur, hi, [], [ev]))
        out.sort(key=lambda t: t[0])
        self.segs[name] = out

    def _add_waits(self, eng, deps):
        kn = self.known[eng]
        kd = self.known_dma[eng]
        best = {}
        dm = set()
        for d in deps:
            if d[0] == 'e':
                _, e2, i2 = d
                if e2 == 'pe' and eng == 'pe':
                    continue
                if kn[e2] >= i2:
                    continue
                if best.get(e2, -1) < i2:
                    best[e2] = i2
            else:
                if d[1] in kd:
                    continue
                dm.add(d[1])
        for e2, i2 in best.items():
            if kn[e2] >= i2:
                continue
            self.insref[e2][i2][2] = True
            self.ops[eng].append(['wait', ('e', e2, i2)])
            ck = self.clock[e2][i2]
            for j, c in enumerate(CENG):
                if ck[j] > kn[c]:
                    kn[c] = ck[j]
            if kn[e2] < i2:
                kn[e2] = i2
        for did in sorted(dm):
            self.ops[eng].append(['wait', ('d', did)])
            kd.add(did)
            ck = self.dmas[did][2]
            for j, c in enumerate(CENG):
                if ck[j] > kn[c]:
                    kn[c] = ck[j]

    def op(self, eng, fn, ins=(), outs=()):
        idx = self.nins[eng]
        ev = ('e', eng, idx)
        deps = []
        for ap in ins:
            r = self._region(ap)
            if r is not None:
                self._access(r, False, ev, deps)
        for ap in outs:
            r = self._region(ap)
            if r is not None:
                self._access(r, True, ev, deps)
        deps = [d for d in deps if d != ev]
        self._add_waits(eng, deps)
        ent = ['ins', fn, False, None]
        self.ops[eng].append(ent)
        self.insref[eng].append(ent)
        kn = self.known[eng]
        ck = tuple(idx if c == eng else kn[c] for c in CENG)
        self.clock[eng].append(ck)
        if eng == 'pe':
            kn['pe'] = idx - 1 if idx > 0 else -1
        self.nins[eng] = idx + 1
        return ev

    def dma(self, q, out, in_, fn=None, **kw):
        did = len(self.dmas)
        k = self.dma_count[q]
        ev = ('d', did)
        deps = []
        r = self._region(in_)
        if r is not None:
            self._access(r, False, ev, deps)
        r = self._region(out)
        if r is not None:
            self._access(r, True, ev, deps)
        if k >= NSEM:
            deps.append(('d', self.dma_ids[q][k - NSEM]))
        deps = [d for d in deps if d != ev]
        self._add_waits(q, deps)
        kn = self.known[q]
        ck = tuple(kn[c] for c in CENG)
        self.dmas.append((q, k, ck))
        self.dma_ids[q].append(did)
        self.dma_count[q] = k + 1

        if fn is None:
            def fn(e, out=out, in_=in_, kw=kw):
                return e.dma_start(out=out, in_=in_, **kw)
        ent = ['ins', fn, False, (q, k)]
        self.ops[q].append(ent)
        if q in CENG:
            pass
        return ev

    def coll(self, fn):
        did = len(self.dmas)
        k = len(self.dma_ids['cc'])
        kn = self.known['pool']
        ck = tuple(kn[c] for c in CENG)
        self.dmas.append(('cc', k, ck))
        self.dma_ids['cc'].append(did)
        self.ops['pool'].append(['ins', fn, False, ('cc', k)])
        return ('d', did)

    def barrier(self):
        last = {c: self.nins[c] - 1 for c in CENG}
        alld = list(range(len(self.dmas)))
        for e in ENGS:
            deps = [('e', c, last[c]) for c in CENG if last[c] >= 0 and c != e]
            if e in CENG and last[e] >= 0 and e != 'pe':
                deps.append(('e', e, last[e]))
            kd = self.known_dma[e]
            for q in ENGS + ('cc',):
                ids = self.dma_ids[q][-NSEM:]
                deps.extend(('d', i) for i in ids)
            self._add_waits(e, deps)
        for e in ENGS:
            self.known_dma[e] = set(alld[-4 * NSEM * len(ENGS):]) | self.known_dma[e]
        self.segs = {}

    def emit(self):
        nc = self.nc
        sems = {c: nc.alloc_semaphore("s_" + c) for c in CENG}
        dsems = {q: [nc.alloc_semaphore("d_%s_%d" % (q, i)) for i in range(NSEM)]
                 for q in ENGS if self.dma_count[q] > 0}
        ccsem = nc.alloc_semaphore("cc_sem") if self.dma_ids['cc'] else None
        cnt = {}
        for c in CENG:
            n = 0
            arr = []
            for ent in self.insref[c]:
                if ent[2]:
                    n += 1
                arr.append(n)
            cnt[c] = arr
        ops = self.ops
        dmas = self.dmas

        def run(engname, e):
            for ent in ops[engname]:
                if ent[0] == 'wait':
                    ev = ent[1]
                    if ev[0] == 'e':
                        e.wait_ge(sems[ev[1]], cnt[ev[1]][ev[2]])
                    else:
                        q, k, _ = dmas[ev[1]]
                        if q == 'cc':
                            e.wait_ge(ccsem, k + 1)
                        else:
                            e.wait_ge(dsems[q][k % NSEM], 16 * (k // NSEM + 1))
                else:
                    ins = ent[1](e)
                    if ent[3] is not None:
                        q, k = ent[3]
                        if q == 'cc':
                            ins.then_inc(ccsem, 1)
                        else:
                            ins.then_inc(dsems[q][k % NSEM], 16)
                    elif ent[2]:
                        ins.then_inc(sems[engname], 1)

        with nc.Block() as block:
            @block.sync
            def _(e):
                run('sp', e)

            @block.scalar
            def _(e):
                run('act', e)

            @block.vector
            def _(e):
                run('dve', e)

            @block.gpsimd
            def _(e):
                run('pool', e)

            @block.tensor
            def _(e):
                run('pe', e)

    def mm(self, out, lhsT, rhs, start=True, stop=True):
        return self.op('pe', lambda e: e.matmul(out, lhsT, rhs, start=start, stop=stop),
                       ins=(lhsT, rhs), outs=(out,))

    def transpose(self, out, in_, ident):
        return self.op('pe', lambda e: e.transpose(out, in_, ident), ins=(in_, ident), outs=(out,))

    def act(self, out, in_, func, scale=1.0, bias=0.0, accum_out=None, eng='act'):
        outs = (out,) if accum_out is None else (out, accum_out)
        insl = [in_]
        if not isinstance(scale, (int, float)):
            insl.append(scale)
        if not isinstance(bias, (int, float)):
            insl.append(bias)
        if accum_out is None:
            f = lambda e: e.activation(out, in_, func, bias=bias, scale=scale)
        else:
            f = lambda e: e.activation(out, in_, func, bias=bias, scale=scale, accum_out=accum_out)
        return self.op('act', f, ins=insl, outs=outs)

    def copy(self, eng, out, in_):
        if eng == 'act':
            return self.op('act', lambda e: e.copy(out, in_), ins=(in_,), outs=(out,))
        return self.op(eng, lambda e: e.tensor_copy(out, in_), ins=(in_,), outs=(out,))

    def tt(self, eng, out, in0, in1, op):
        return self.op(eng, lambda e: e.tensor_tensor(out, in0, in1, op), ins=(in0, in1), outs=(out,))

    def ts(self, eng, out, in0, s1, op0, s2=None, op1=None):
        insl = [in0]
        if not isinstance(s1, (int, float)):
            insl.append(s1)
        if s2 is not None and not isinstance(s2, (int, float)):
            insl.append(s2)
        if op1 is None:
            f = lambda e: e.tensor_scalar(out, in0, s1, None, op0)
        else:
            f = lambda e: e.tensor_scalar(out, in0, s1, s2, op0, op1)
        return self.op(eng, f, ins=insl, outs=(out,))

    def stt(self, eng, out, in0, scalar, in1, op0, op1):
        insl = [in0, in1]
        if not isinstance(scalar, (int, float)):
            insl.append(scalar)
        return self.op(eng, lambda e: e.scalar_tensor_tensor(out, in0, scalar, in1, op0, op1),
                       ins=insl, outs=(out,))

    def recip(self, out, in_):
        return self.op('dve', lambda e: e.reciprocal(out, in_), ins=(in_,), outs=(out,))

    def memset(self, eng, ap, val):
        return self.op(eng, lambda e: e.memset(ap, val), ins=(), outs=(ap,))


def sap(t, off, pat, np_=128):
    base = t[:]
    p0 = base.ap[0]
    return bass.AP(t, off, [[p0[0], np_]] + [list(x) for x in pat])


EPS = 1e-6
PAD = 1024
DILS = (1, 4, 16)
SCALE = 128 ** -0.5
G = 512


class Cfg:
    def __init__(self, D=2048, FF=5632, T=8192, NSEG=2, DEPTH=4, HALO=False):
        self.D, self.FF, self.T, self.NSEG, self.DEPTH = D, FF, T, NSEG, DEPTH
        self.HALO = HALO
        self.NCH = D // 128
        self.NFF = FF // 128
        self.NG = T // G
        self.TP = T + 2 * PAD
        self.TSEG = T // NSEG


class Arena:
    def __init__(self, nc, base, limit, prefix):
        self.nc, self.off, self.limit, self.prefix = nc, base, limit, prefix

    def t(self, name, shape, dtype):
        nb = 4 if dtype == F32 else 2
        n = 1
        for s in shape[1:]:
            n *= s
        size = (n * nb + 63) // 64 * 64
        h = self.nc.alloc_sbuf_tensor_at(self.prefix + name, list(shape), dtype, offset=self.off)
        self.off += size
        assert self.off <= self.limit, (self.prefix + name, self.off, self.limit)
        return h


class Streamer:
    def __init__(self, P, ring, ns, blocks, q='sp', pf=None):
        self.P, self.ring, self.ns, self.blocks, self.q = P, ring, ns, blocks, q
        self.pf = pf if pf is not None else ns - 4
        self.issued = 0

    def slot(self, i):
        return i % self.ns

    def get(self, i):
        upto = min(len(self.blocks), i + self.pf + 1)
        while self.issued < upto:
            j = self.issued
            src = self.blocks[j]
            n = src.shape[-1]
            dst = self.ring[:, j % self.ns, 0:n]
            self.P.dma(self.q, dst, src)
            self.issued += 1
        return i % self.ns


def dap(t_ap, off, pat):
    return bass.AP(t_ap.tensor, off, [list(x) for x in pat])


def build(cfg):
    nc = bass.Bass("TRN2", target_bir_lowering=False)
    P = Prog(nc)
    D, FF, T, NSEG, L = cfg.D, cfg.FF, cfg.T, cfg.NSEG, cfg.DEPTH
    NCH, NFF, NG, TP, TSEG = cfg.NCH, cfg.NFF, cfg.NG, cfg.TP, cfg.TSEG
    KW = NCH * 128

    def din(name, shape, dt=F32):
        return nc.dram_tensor(name, list(shape), dt, kind="ExternalInput").ap()

    def dscr(name, shape, dt):
        return nc.dram_tensor(name, list(shape), dt, kind="Internal").ap()

    x_d = din("x", [T, D])
    mem_d = din("mem", [NSEG, 256, D])
    cos_d = din("cos", [128, T])
    sin_d = din("sin", [128, T])
    masks_d = din("masks", [128, 4, 256])
    ones_d = din("ones_f", [128, 128])
    rt_d = din("rt_f", [128, 128])
    ident_d = din("ident", [128, 128])
    gmix_d = din("g_mix", [128, L, NCH])
    gmem_d = din("g_mem", [128, L, NCH])
    gffn_d = din("g_ffn", [128, L, NCH])
    ggo_d = din("g_go", [128, L, 16])
    gfin_d = din("g_fin", [128, D])
    gsgu_d = din("g_sgu", [128, L, 512])
    bsp_d = din("b_sp", [128, L, 512])
    wsp_d = din("w_spT", [128, L, 4, 128])
    win_d = din("w_in_t", [L, 36, 128, KW])
    wout_d = din("w_out_t", [L, NCH, 128, 2048])
    wgu_d = din("w_gu_t", [L, 2 * NFF, 128, KW])
    wdn_d = din("w_dn_t", [L, NCH, 128, NFF * 128])
    wkv_d = din("w_kv_t", [L, 8, 128, KW])
    y_d = nc.dram_tensor("y", [T, D], F32, kind="ExternalOutput").ap()

    wb_in = dscr("wb_in", [L, 36, 128, KW], BF16)
    wb_out = dscr("wb_out", [L, NCH, 128, 2048], BF16)
    wb_gu = dscr("wb_gu", [L, 2 * NFF, 128, KW], BF16)
    wb_dn = dscr("wb_dn", [L, NCH, 128, NFF * 128], BF16)
    wb_kv = dscr("wb_kv", [L, 8, 128, KW], BF16)
    xT_s = dscr("xT_s", [NCH, 128, T], F32)
    KT_s = dscr("KT_s", [8, 128, TP], BF16)
    QT_s = dscr("QT_s", [8, 128, T], BF16)
    V_s = dscr("V_s", [8, TP, 128], BF16)
    bT_s = dscr("bT_s", [8, 128, T], BF16)
    mA_s = dscr("mA_s", [4, 128, T], BF16)
    mC_s = dscr("mC_s", [4, 128, T], BF16)
    if cfg.HALO:
        snd = [dscr("snd%d" % i, [1024, 1024], BF16) for i in range(4)]
        rcv = [dscr("rcv%d" % i, [2 * 1024, 1024], BF16) for i in range(4)]

    SB_LIMIT = 229344
    CA = Arena(nc, 16512, SB_LIMIT, "c_")
    ones_b = CA.t("ones", [128, 128], BF16)
    rt_b = CA.t("rt", [128, 128], BF16)
    ident = CA.t("ident", [128, 128], F32)
    masks_b = CA.t("masks", [128, 4, 256], BF16)
    gmix = CA.t("gmix", [128, L, NCH], F32)
    gmem = CA.t("gmem", [128, L, NCH], F32)
    gffn = CA.t("gffn", [128, L, NCH], F32)
    ggo = CA.t("ggo", [128, L, 16], F32)
    memKT = CA.t("memKT", [128, NSEG, 4, 256], BF16)
    memV = CA.t("memV", [128, NSEG, 2, 512], BF16)
    gsgu = CA.t("gsgu", [128, 512], F32)
    bsp = CA.t("bsp", [128, 512], F32)
    wspT = CA.t("wspT", [128, 4, 128], BF16)
    ctmp = CA.t("ctmp", [128, 1024], F32)
    zpad = CA.t("zpad", [128, 1024], BF16)
    C0 = CA.off

    ps = [nc.alloc_psum_tensor("ps%d" % i, [128, 512], F32) for i in range(8)]

    def cast_list(l):
        out = []
        for (dst, src, n) in ((wb_kv, wkv_d, 8), (wb_in, win_d, 36), (wb_out, wout_d, NCH),
                              (wb_gu, wgu_d, 2 * NFF), (wb_dn, wdn_d, NCH)):
            step = 4
            for o in range(0, n, step):
                e = min(n, o + step)
                out.append((dst[l, o:e], src[l, o:e]))
        return out

    def init():
        P.dma('sp', ctmp[:, 0:128], ones_d)
        P.copy('dve', ones_b[:], ctmp[:, 0:128])
        P.dma('sp', ctmp[:, 128:256], rt_d)
        P.copy('dve', rt_b[:], ctmp[:, 128:256])
        P.dma('sp', ident[:], ident_d)
        P.dma('sp', ctmp[:, 0:1024].rearrange("p (a b) -> p a b", a=4), masks_d)
        P.copy('dve', masks_b[:], ctmp[:, 0:1024].rearrange("p (a b) -> p a b", a=4))
        P.dma('sp', gmix[:], gmix_d)
        P.dma('sp', gmem[:], gmem_d)
        P.dma('sp', gffn[:], gffn_d)
        P.dma('sp', ggo[:], ggo_d)
        for (dst, src) in cast_list(0):
            P.dma('pool', dst, src)
        z = zpad
        P.memset('dve', z[:], 0.0)
        for h in range(8):
            P.dma('sp', KT_s[h, :, 0:PAD], z[:, 0:PAD])
            P.dma('sp', KT_s[h, :, PAD + T:TP], z[:, 0:PAD])
            for b0 in (0, PAD + T):
                P.dma('sp', V_s[h, b0:b0 + PAD, :].rearrange("(a p) d -> p a d", p=128),
                      z[:, 0:1024].rearrange("p (a d) -> p a d", d=128))

    def norm_fm(src, ncs, N, gcol, dst, nfeat, sq, ps_ss, srt, rstd, eng_mul=('dve',)):
        for c in range(ncs):
            P.act(sq[:, c, 0:N], src(c), AF.Square)
        for c in range(ncs):
            P.mm(ps_ss[:, 0:N], ones_b[:], sq[:, c, 0:N], start=(c == 0), stop=(c == ncs - 1))
        P.act(srt[:, 0:N], ps_ss[:, 0:N], AF.Sqrt, scale=1.0 / nfeat, bias=EPS)
        P.recip(rstd[:, 0:N], srt[:, 0:N])
        for c in range(ncs):
            P.stt('dve', dst(c), src(c), gcol(c), rstd[:, 0:N], ALU.mult, ALU.mult)

    def pass0():
        A = Arena(nc, C0, SB_LIMIT, "p0_")
        xt = [A.t("x%d" % i, [128, D], F32) for i in range(2)]
        stg = [A.t("s%d" % i, [128, NCH, 128], F32) for i in range(2)]
        nb = 0
        for tt in range(T // 128):
            xs = xt[tt % 2]
            sg = stg[tt % 2]
            P.dma('sp', xs[:], x_d[tt * 128:(tt + 1) * 128, :])
            for c0 in range(0, NCH, 4):
                bank = ps[nb % 4]
                nb += 1
                nn = min(4, NCH - c0)
                for c in range(c0, c0 + nn):
                    P.transpose(bank[:, (c - c0) * 128:(c - c0 + 1) * 128], xs[:, c * 128:(c + 1) * 128], ident[:])
                P.copy('act' if (nb % 2) else 'dve', sg[:, c0:c0 + nn, :],
                       bank[:, 0:nn * 128].rearrange("p (a b) -> p a b", b=128))
            P.dma('act', xT_s[:, :, tt * 128:(tt + 1) * 128].rearrange("c p t -> p c t"), sg[:])

    def passM(l):
        A = Arena(nc, C0, SB_LIMIT, "pm_")
        mt_ = A.t("mem", [128, 2, D], F32)
        memT = A.t("memT", [128, NCH, 256], F32)
        sq = A.t("sq", [128, max(NCH, 8), 256], BF16)
        memn = A.t("memn", [128, NCH, 256], BF16)
        srt = A.t("srt", [128, 256], F32)
        rstd = A.t("rstd", [128, 256], F32)
        wkv = A.t("wkv", [128, 8, KW], BF16)
        tmpf = A.t("tmpf", [128, 512], F32)
        P.dma('sp', wkv[:], wb_kv[l].rearrange("o p n -> p o n"))
        P.dma('sp', gsgu[:], gsgu_d[:, l, :])
        P.dma('sp', bsp[:], bsp_d[:, l, :])
        P.dma('sp', tmpf[:].rearrange("p (a b) -> p a b", a=4), wsp_d[:, l])
        P.copy('dve', wspT[:], tmpf[:].rearrange("p (a b) -> p a b", a=4))
        nb = 0
        for s in range(NSEG):
            P.dma('sp', mt_[:], mem_d[s].rearrange("(a p) d -> p a d", p=128))
            for a in range(2):
                for c0 in range(0, NCH, 4):
                    bank = ps[nb % 2]
                    nb += 1
                    nn = min(4, NCH - c0)
                    for c in range(c0, c0 + nn):
                        P.transpose(bank[:, (c - c0) * 128:(c - c0 + 1) * 128], mt_[:, a, c * 128:(c + 1) * 128], ident[:])
                    P.copy('dve', memT[:, c0:c0 + nn, a * 128:(a + 1) * 128],
                           bank[:, 0:nn * 128].rearrange("p (a b) -> p a b", b=128))
            norm_fm(lambda c: memT[:, c, :], NCH, 256, lambda c: gmem[:, l, c:c + 1],
                    lambda c: memn[:, c, :], D, sq, ps[2], srt, rstd)
            for hc in range(4):
                bank = ps[3 + hc % 2]
                for kc in range(NCH):
                    P.mm(bank[:, 0:256], wkv[:, hc, kc * 128:(kc + 1) * 128], memn[:, kc, :],
                         start=(kc == 0), stop=(kc == NCH - 1))
                P.copy('act', memKT[:, s, hc, :], bank[:, 0:256])
            for m in range(2):
                bank = ps[5 + m % 2]
                for kc in range(NCH):
                    P.mm(bank[:], memn[:, kc, m * 128:(m + 1) * 128],
                         sap(wkv, 4 * KW + kc * 128, [[KW, 4], [1, 128]]),
                         start=(kc == 0), stop=(kc == NCH - 1))
                P.copy('act', memV[:, s, m, :], bank[:])

    def passA(l):
        A = Arena(nc, C0, SB_LIMIT, "pa_")
        xT = A.t("xT", [128, NCH, G], F32)
        sq = A.t("sq", [128, max(NCH, 8), G], BF16)
        hT = A.t("hT", [128, NCH, G], BF16)
        srt = A.t("srt", [128, G], F32)
        rstd = A.t("rstd", [128, G], F32)
        NS = 8
        ring = A.t("ring", [128, NS, KW], BF16)
        cs = [A.t("cos%d" % i, [128, G], F32) for i in range(2)]
        sn = [A.t("sin%d" % i, [128, G], F32) for i in range(2)]
        uT = A.t("uT", [128, 4, G], F32)
        vtm = A.t("vtm", [128, 4, 512], F32)
        vtmp = A.t("vtmp", [128, 512], F32)
        vv = A.t("vv", [128, 4, 512], BF16)
        ssq = A.t("ssq", [128, 16], F32)
        ssr = A.t("ssr", [128, 16], F32)
        junk = A.t("junk", [128, 128], BF16)
        aT = A.t("aT", [128, 4, G], F32)
        atmp = [A.t("atmp%d" % i, [128, G], F32) for i in range(2)]
        amix = A.t("amix", [128, 4, G], BF16)
        qcT = A.t("qcT", [128, 4, G], BF16)
        PT = [A.t("PT%d" % i, [128, G], BF16) for i in range(2)]
        rden = A.t("rden", [128, G], F32)
        cT = A.t("cT", [128, 4, G], F32)
        cmix = A.t("cmix", [128, 4, G], BF16)
        zb = [A.t("zb%d" % i, [128, G], BF16) for i in range(2)]
        t1 = [A.t("t1_%d" % i, [128, G], F32) for i in range(2)]
        t2 = [A.t("t2_%d" % i, [128, G], F32) for i in range(2)]
        ro = [A.t("ro%d" % i, [128, G], BF16) for i in range(2)]
        vout = [A.t("vout%d" % i, [128, 1024], BF16) for i in range(4)]

        order = list(range(0, 4)) + list(range(4, 8)) + list(range(32, 36)) + list(range(8, 32))
        blocks = []
        for g in range(NG):
            for oc in order:
                blocks.append(wb_in[l, oc])
        st = Streamer(P, ring, NS, blocks, 'sp')
        psA = [ps[0], ps[1], ps[2]]
        ps_ss = ps[3]
        psm = [ps[4], ps[5]]
        ps_num, ps_den = ps[6], ps[7]
        cnt = {'a': 0, 'm': 0, 'z': 0}

        def nextA():
            b = psA[cnt['a'] % 3]
            cnt['a'] += 1
            return b

        def nextM():
            b = psm[cnt['m'] % 2]
            cnt['m'] += 1
            return b

        for g in range(NG):
            t0 = g * G
            seg = t0 // TSEG
            bi = g * 36
            cg, sg_ = cs[g % 2], sn[g % 2]
            if g == 0:
                P.dma('pool', xT[:], xT_s[:, :, t0:t0 + G].rearrange("c p t -> p c t"))
                P.dma('pool', cg[:], cos_d[:, t0:t0 + G])
                P.dma('pool', sg_[:], sin_d[:, t0:t0 + G])
            norm_fm(lambda c: xT[:, c, :], NCH, G, lambda c: gmix[:, l, c:c + 1],
                    lambda c: hT[:, c, :], D, sq, ps_ss, srt, rstd, eng_mul=('dve', 'pool'))
            if g + 1 < NG:
                t1n = t0 + G
                P.dma('pool', xT[:], xT_s[:, :, t1n:t1n + G].rearrange("c p t -> p c t"))
                P.dma('pool', cs[(g + 1) % 2][:], cos_d[:, t1n:t1n + G])
                P.dma('pool', sn[(g + 1) % 2][:], sin_d[:, t1n:t1n + G])

            def proj_fm(bidx):
                s_ = st.get(bidx)
                bank = nextA()
                for kc in range(NCH):
                    P.mm(bank[:], ring[:, s_, kc * 128:(kc + 1) * 128], hT[:, kc, :],
                         start=(kc == 0), stop=(kc == NCH - 1))
                return bank

            def proj_tm(bidx0, tt):
                s0 = st.get(bidx0)
                for j in range(1, 4):
                    st.get(bidx0 + j)
                assert s0 % 4 == 0
                bank = nextA()
                for kc in range(NCH):
                    P.mm(bank[:], hT[:, kc, tt * 128:(tt + 1) * 128],
                         sap(ring, s0 * KW + kc * 128, [[KW, 4], [1, 128]]),
                         start=(kc == 0), stop=(kc == NCH - 1))
                return bank

            for i in range(4):
                bank = proj_fm(bi + i)
                P.act(uT[:, i, :], bank[:], AF.Gelu_apprx_tanh)
            P.memset('pool', ssq[:], 0.0)
            for tt in range(4):
                bank = proj_tm(bi + 4, tt)
                P.act(vtm[:, tt, :], bank[:], AF.Gelu_apprx_tanh)
                for gi in range(4):
                    P.act(junk[:], vtm[:, tt, gi * 128:(gi + 1) * 128], AF.Square,
                          accum_out=ssq[:, tt * 4 + gi:tt * 4 + gi + 1])
            P.act(ssr[:], ssq[:], AF.Sqrt, scale=1.0 / 128, bias=EPS)
            P.recip(ssq[:], ssr[:])
            for tt in range(4):
                P.tt('dve', vtmp[:].rearrange("p (a b) -> p a b", a=4),
                     vtm[:, tt, :].rearrange("p (a b) -> p a b", a=4),
                     sap(ssq, tt * 4, [[1, 4], [0, 128]]), ALU.mult)
                P.tt('pool', vv[:, tt, :], vtmp[:], gsgu[:], ALU.mult)
            for i in range(4):
                bank = proj_fm(bi + 8 + i)
                P.copy('act', qcT[:, i, :], bank[:])
            for gi in range(4):
                bank = nextM()
                for tt in range(4):
                    P.mm(bank[:, tt * 128:(tt + 1) * 128], vv[:, tt, gi * 128:(gi + 1) * 128], wspT[:, gi, :])
                at = atmp[gi % 2]
                P.tt('dve', at[:].rearrange("p (a b) -> p a b", a=4),
                     bank[:].rearrange("p (a b) -> p a b", a=4),
                     sap(bsp, gi * 128, [[0, 4], [1, 128]]), ALU.add)
                P.tt('pool', aT[:, gi, :], at[:], uT[:, gi, :], ALU.mult)
            norm_fm(lambda c: aT[:, c, :], 4, G, lambda c: ggo[:, l, c:c + 1],
                    lambda c: amix[:, c, :], 512, sq, ps_ss, srt, rstd, eng_mul=('pool',))
            P.dma('pool', mA_s[:, :, t0:t0 + G].rearrange("c p t -> p c t"), amix[:])
            def rope_a(i):
                bank = proj_fm(bi + 12 + i)
                k = cnt['z'] % 2
                cnt['z'] += 1
                P.copy('act', zb[k][:], bank[:])
                return (i, k)

            def rope_b(stt_):
                i, k = stt_
                h = i % 8
                isk = i >= 8
                z, o, a1, a2 = zb[k], ro[k], t1[k], t2[k]
                b2 = nextM()
                P.mm(b2[:], rt_b[:], z[:])
                P.tt('dve', a1[:], z[:], cg[:], ALU.mult)
                P.tt('dve', a2[:], b2[:], sg_[:], ALU.mult)
                P.tt('pool', o[:], a1[:], a2[:], ALU.add)
                if isk:
                    P.dma('pool', KT_s[h, :, PAD + t0:PAD + t0 + G], o[:])
                else:
                    P.dma('pool', QT_s[h, :, t0:t0 + G], o[:])

            def cross_a(hc):
                for m_ in range(2):
                    bank = nextM()
                    P.mm(bank[:], memKT[:, seg, hc, m_ * 128:(m_ + 1) * 128], qcT[:, hc, :])
                    P.act(PT[m_][:], bank[:], AF.Exp, scale=SCALE)

            def cross_b(hc):
                for m_ in range(2):
                    P.mm(ps_num[:], memV[:, seg, m_, hc * 128:(hc + 1) * 128], PT[m_][:], start=(m_ == 0), stop=(m_ == 1))
                for m_ in range(2):
                    P.mm(ps_den[:], ones_b[:], PT[m_][:], start=(m_ == 0), stop=(m_ == 1))
                P.recip(rden[:], ps_den[:])
                P.tt('dve', cT[:, hc, :], ps_num[:], rden[:], ALU.mult)

            pend = None
            for i in range(16):
                cur = rope_a(i)
                if pend is not None:
                    rope_b(pend)
                pend = cur
                if i in (1, 3, 5, 7):
                    cross_a((i - 1) // 2)
                if i in (2, 4, 6, 8):
                    cross_b((i - 2) // 2)
                if i == 10:
                    norm_fm(lambda c: cT[:, c, :], 4, G, lambda c: ggo[:, l, 12 + c:13 + c],
                            lambda c: cmix[:, c, :], 512, sq, ps_ss, srt, rstd, eng_mul=('pool',))
                    P.dma('pool', mC_s[:, :, t0:t0 + G].rearrange("c p t -> p c t"), cmix[:])
            rope_b(pend)
            for blk in range(2):
                for tt in range(4):
                    bank = proj_tm(bi + 28 + blk * 4, tt)
                    P.copy('act' if tt % 2 == 0 else 'dve', vout[tt][:, blk * 512:(blk + 1) * 512], bank[:])
            for tt in range(4):
                tk = PAD + t0 + tt * 128
                P.dma('pool', V_s[:, tk:tk + 128, :].rearrange("h p d -> p h d"),
                      vout[tt][:].rearrange("p (h d) -> p h d", h=8))

    def passB(l):
        A = Arena(nc, C0, SB_LIMIT, "pb_")
        KThs = [A.t("KTh%d" % i, [128, TP], BF16) for i in range(2)]
        QThs = [A.t("QTh%d" % i, [128, T], BF16) for i in range(2)]
        accns = [A.t("accn%d" % i, [128, T], F32) for i in range(2)]
        accds = [A.t("accd%d" % i, [128, T], F32) for i in range(2)]
        bTh = A.t("bTh", [128, T], BF16)
        NTV = 17
        NVS = 4
        vt = A.t("vt", [128, NVS, NTV * 128], BF16)
        psN = [ps[2], ps[3]]
        psD = [ps[4], ps[5]]
        cnt = {'s': 0, 'v': 0, 'p': 0}
        jobs = []
        for d in DILS:
            Lc = T // d
            nq = Lc // 128
            for r in range(d):
                i = 0
                while i <= nq:
                    n = min(NTV, nq + 1 - i)
                    jobs.append((d, r, i, n, nq))
                    i += n
        LA = 3
        NPT = 6
        PTt = [A.t("PTx%d" % i, [128, 256], BF16) for i in range(NPT)]
        psS = [ps[0], ps[1], ps[6], ps[7]]
        tiles = []
        for jn, (d, r, i0, n, nq) in enumerate(jobs):
            for ii in range(n):
                tiles.append((jn, d, r, i0 + ii, ii, nq))
        ntl = len(tiles)

        def vload(h, jn):
            d, r, i0, n, nq = jobs[jn]
            vs = jn % NVS
            off = (PAD + r + d * (128 * i0 - 64)) * 128
            src = dap(V_s[h], V_s[h].offset + off, [[d * 128, 128], [d * 128 * 128, n], [1, 128]])
            P.dma('sp', vt[:, vs, 0:n * 128].rearrange("p (a b) -> p a b", b=128), src)

        P.dma('sp', KThs[0][:], KT_s[0])
        P.dma('sp', QThs[0][:], QT_s[0])
        for h in range(8):
            KTh, QTh, accn, accd = KThs[h % 2], QThs[h % 2], accns[h % 2], accds[h % 2]
            if h + 1 < 8:
                P.dma('sp', KThs[(h + 1) % 2][:], KT_s[h + 1])
                P.dma('sp', QThs[(h + 1) % 2][:], QT_s[h + 1])
            vload(h, 0)
            for idx in range(ntl + LA):
                if idx < ntl:
                    jn, d, r, i, ii, nq = tiles[idx]
                    if ii == 0 and jn + 1 < len(jobs):
                        vload(h, jn + 1)
                    c0 = 128 if i == 0 else 0
                    c1 = 128 if i == nq else 256
                    kind = 0
                    if i == 0:
                        kind = 1
                    elif i == nq:
                        kind = 2
                    elif NSEG == 2 and i == nq // 2:
                        kind = 3
                    bS = psS[idx % 4]
                    pt = PTt[idx % NPT]
                    P.mm(bS[:, c0:c1], sap(KTh, PAD + r + d * (128 * i - 64), [[d, 128]]),
                         sap(QTh, r + d * (128 * (i - 1) + c0), [[d, c1 - c0]]))
                    P.act(pt[:, c0:c1], bS[:, c0:c1], AF.Exp, scale=SCALE)
                    P.tt('pool', pt[:, c0:c1], pt[:, c0:c1], masks_b[:, kind, c0:c1], ALU.mult)
                k2 = idx - LA
                if k2 >= 0:
                    jn, d, r, i, ii, nq = tiles[k2]
                    vs = jn % NVS
                    pt = PTt[k2 % NPT]
                    for hh in (0, 1):
                        j = i - 1 + hh
                        if j < 0 or j >= nq:
                            continue
                        P.mm(psN[j % 2][:, 0:128], vt[:, vs, ii * 128:(ii + 1) * 128], pt[:, hh * 128:(hh + 1) * 128],
                             start=(hh == 1), stop=(hh == 0))
                        P.mm(psD[j % 2][:, 0:128], ones_b[:], pt[:, hh * 128:(hh + 1) * 128],
                             start=(hh == 1), stop=(hh == 0))
                        if hh == 0:
                            tok = r + d * 128 * j
                            an = sap(accn, tok, [[d, 128]])
                            ad = sap(accd, tok, [[d, 128]])
                            if d == 1:
                                P.copy('dve', an, psN[j % 2][:, 0:128])
                                P.copy('act', ad, psD[j % 2][:, 0:128])
                            else:
                                P.tt('dve', an, psN[j % 2][:, 0:128], an, ALU.add)
                                P.tt('dve', ad, psD[j % 2][:, 0:128], ad, ALU.add)
            CH = 2048
            for c in range(0, T, CH):
                P.recip(accd[:, c:c + CH], accd[:, c:c + CH])
                P.tt('pool', bTh[:, c:c + CH], accn[:, c:c + CH], accd[:, c:c + CH], ALU.mult)
                P.dma('pool', bT_s[h, :, c:c + CH], bTh[:, c:c + CH])

    def passC(l):
        last = (l == L - 1)
        A = Arena(nc, C0, SB_LIMIT, "pc_")
        xT = A.t("xT", [128, NCH, G], F32)
        mixT = A.t("mixT", [128, 16, G], BF16)
        sq = A.t("sq", [128, max(NCH, 8), G], BF16)
        hT = A.t("hT", [128, NCH, G], BF16)
        srt = A.t("srt", [128, G], F32)
        rstd = A.t("rstd", [128, G], F32)
        actT = A.t("actT", [128, NFF, G], BF16)
        NS = 6 if last else 8
        ring = A.t("ring", [128, NS, 2048], BF16)
        sgt = [A.t("sg%d" % i, [128, G], F32) for i in range(2)]
        if last:
            xtm = [A.t("xtm%d" % i, [128, D], F32) for i in range(1)]
            ytm = [A.t("ytm%d" % i, [128, D], F32) for i in range(1)]
            gfin = A.t("gfin", [128, D], F32)
            ssq = A.t("ssq", [128, 8], F32)
            ssr = A.t("ssr", [128, 8], F32)
            P.dma('act', gfin[:], gfin_d)
        nparts = (NFF + 15) // 16
        blocks = []
        for g in range(NG):
            for oc in range(NCH):
                blocks.append(wb_out[l, oc])
            for j in range(NFF):
                blocks.append(wb_gu[l, j])
                blocks.append(wb_gu[l, NFF + j])
            for oc in range(NCH):
                for pt_ in range(nparts):
                    k0 = pt_ * 16
                    k1 = min(NFF, k0 + 16)
                    blocks.append(wb_dn[l, oc, :, k0 * 128:k1 * 128])
        st = Streamer(P, ring, NS, blocks, 'sp', pf=NS - 3)
        nblk = NCH + 2 * NFF + NCH * nparts
        psA = [ps[0], ps[1], ps[2], ps[3], ps[4], ps[5]]
        ps_ss = ps[6]
        psT = ps[7]
        cnt = {'a': 0, 'f': 0}

        def nextA():
            b = psA[cnt['a'] % 6]
            cnt['a'] += 1
            return b

        bgc = cast_list(l + 1) if l + 1 < L else []
        per = (len(bgc) + NG - 1) // NG if bgc else 0
        for g in range(NG):
            t0 = g * G
            bi = g * nblk
            for (dst_, src_) in bgc[g * per:(g + 1) * per]:
                P.dma('pool', dst_, src_)
            P.dma('act', xT[:], xT_s[:, :, t0:t0 + G].rearrange("c p t -> p c t"))
            P.dma('act', mixT[:, 0:4, :], mA_s[:, :, t0:t0 + G].rearrange("c p t -> p c t"))
            P.dma('act', mixT[:, 4:12, :], bT_s[:, :, t0:t0 + G].rearrange("c p t -> p c t"))
            P.dma('act', mixT[:, 12:16, :], mC_s[:, :, t0:t0 + G].rearrange("c p t -> p c t"))
            norm_fm(lambda c: mixT[:, 4 + c, :], 8, G, lambda c: ggo[:, l, 4 + c:5 + c],
                    lambda c: mixT[:, 4 + c, :], 1024, sq, ps_ss, srt, rstd, eng_mul=('dve', 'pool'))
            for oc in range(NCH):
                s_ = st.get(bi + oc)
                bank = nextA()
                for kc in range(16):
                    P.mm(bank[:], ring[:, s_, kc * 128:(kc + 1) * 128], mixT[:, kc, :],
                         start=(kc == 0), stop=(kc == 15))
                P.tt('dve', xT[:, oc, :], bank[:], xT[:, oc, :], ALU.add)
            norm_fm(lambda c: xT[:, c, :], NCH, G, lambda c: gffn[:, l, c:c + 1],
                    lambda c: hT[:, c, :], D, sq, ps_ss, srt, rstd, eng_mul=('dve', 'pool'))
            b0 = bi + NCH
            for j in range(NFF):
                sg_ = st.get(b0 + 2 * j)
                su_ = st.get(b0 + 2 * j + 1)
                bg = nextA()
                for kc in range(NCH):
                    P.mm(bg[:], ring[:, sg_, kc * 128:(kc + 1) * 128], hT[:, kc, :],
                         start=(kc == 0), stop=(kc == NCH - 1))
                bu = nextA()
                for kc in range(NCH):
                    P.mm(bu[:], ring[:, su_, kc * 128:(kc + 1) * 128], hT[:, kc, :],
                         start=(kc == 0), stop=(kc == NCH - 1))
                sgb = sgt[j % 2]
                P.act(sgb[:], bg[:], AF.Silu)
                P.tt('dve', actT[:, j, :], bu[:], sgb[:], ALU.mult)
            b1 = b0 + 2 * NFF
            for oc in range(NCH):
                bank = nextA()
                for pt_ in range(nparts):
                    s_ = st.get(b1 + oc * nparts + pt_)
                    k0 = pt_ * 16
                    k1 = min(NFF, k0 + 16)
                    for kc in range(k0, k1):
                        P.mm(bank[:], ring[:, s_, (kc - k0) * 128:(kc - k0 + 1) * 128], actT[:, kc, :],
                             start=(kc == 0), stop=(kc == NFF - 1))
                P.tt('dve', xT[:, oc, :], bank[:], xT[:, oc, :], ALU.add)
            if not last:
                P.dma('act', xT_s[:, :, t0:t0 + G].rearrange("c p t -> p c t"), xT[:])
            else:
                for tt in range(4):
                    xm = xtm[0]
                    ym = ytm[0]
                    for c0 in range(0, NCH, 4):
                        nn = min(4, NCH - c0)
                        for c in range(c0, c0 + nn):
                            P.transpose(psT[:, (c - c0) * 128:(c - c0 + 1) * 128], xT[:, c, tt * 128:(tt + 1) * 128], ident[:])
                        P.copy('act' if (c0 // 4) % 2 == 0 else 'dve', xm[:, c0 * 128:(c0 + nn) * 128], psT[:, 0:nn * 128])
                    P.memset('pool', ssq[:, 0:1], 0.0)
                    P.act(sap(sq, 0, [[1, D]]), xm[:], AF.Square, accum_out=ssq[:, 0:1])
                    P.act(ssr[:, 0:1], ssq[:, 0:1], AF.Sqrt, scale=1.0 / D, bias=EPS)
                    P.recip(ssq[:, 1:2], ssr[:, 0:1])
                    P.stt('dve', ym[:], xm[:], ssq[:, 1:2], gfin[:], ALU.mult, ALU.mult)
                    P.dma('act', y_d[t0 + tt * 128:t0 + (tt + 1) * 128, :], ym[:])

    def kview(buf, r0):
        return buf[r0:r0 + 1024, :].rearrange("(h p) t -> h p t", h=8)

    def vview(buf, r0):
        return buf[r0:r0 + 1024, :].rearrange("(h a) (b d) -> h (a b) d", h=8, b=8)

    def exchange(l):
        P.dma('sp', kview(snd[0], 0), KT_s[:, :, PAD:PAD + 1024])
        P.dma('sp', vview(snd[1], 0), V_s[:, PAD:PAD + 1024, :])
        P.dma('sp', kview(snd[2], 0), KT_s[:, :, PAD + T - 1024:PAD + T])
        P.dma('sp', vview(snd[3], 0), V_s[:, PAD + T - 1024:PAD + T, :])
        P.barrier()
        groups = [[0, 1], [2, 3], [4, 5], [6, 7]]
        for i in range(4):
            P.coll(lambda e, i=i: e.collective_compute(
                "AllGather", ALU.bypass, replica_groups=groups, ins=[snd[i]], outs=[rcv[i]]))
            P.barrier()
        P.dma('sp', KT_s[:, :, 0:PAD], kview(rcv[2], 0))
        P.dma('sp', V_s[:, 0:PAD, :], vview(rcv[3], 0))
        P.dma('sp', KT_s[:, :, PAD + T:TP], kview(rcv[0], 1024))
        P.dma('sp', V_s[:, PAD + T:TP, :], vview(rcv[1], 1024))

    init()
    pass0()
    P.barrier()
    for l in range(L):
        passM(l)
        P.barrier()
        passA(l)
        P.barrier()
        if cfg.HALO:
            exchange(l)
            P.barrier()
        passB(l)
        P.barrier()
        passC(l)
        P.barrier()
    P.emit()
    return nc


def rope_tables(pos):
    inv = (np.float32(500000.0) ** (-np.arange(0, 32, 2, dtype=np.float32) / np.float32(32))).astype(np.float32)
    ang = pos.astype(np.float32)[:, None] * inv[None, :]
    c = np.cos(ang).astype(np.float32).T
    s = np.sin(ang).astype(np.float32).T
    T = pos.shape[0]
    cos = np.ones((128, T), np.float32)
    sin = np.zeros((128, T), np.float32)
    cos[0:16] = c
    cos[16:32] = c
    sin[0:16] = s
    sin[16:32] = s
    return cos, sin


def make_masks(split_mid, left_nb=False, right_nb=False):
    kk = np.arange(128)[:, None]
    c = np.arange(256)[None, :]
    band = ((c >= kk) & (c <= kk + 128))
    m = np.zeros((128, 4, 256), np.float32)
    m[:, 0] = band
    m[:, 1] = band if left_nb else (band & (kk >= 64))
    m[:, 2] = band if right_nb else (band & (kk < 64))
    if split_mid:
        m[:, 3] = band & ((kk < 64) == (c < 128))
    else:
        m[:, 3] = band
    return m


def tile_w(w, kdim):
    Lw, K, N = w.shape
    a = w.reshape(Lw, K // 128, 128, N // 128, 128)
    return np.ascontiguousarray(a.transpose(0, 3, 2, 1, 4)).reshape(Lw, N // 128, 128, K)


def const_inputs(cfg, g_mix_norm, w_in, g_sgu, w_spatial, b_spatial, g_mem_norm, w_mem_kv,
                 g_group_out, w_out, g_ffn_norm, w_gate_up, w_down, g_final):
    L = cfg.DEPTH
    f = lambda a: np.ascontiguousarray(np.asarray(a, dtype=np.float32))

    def cols(g):
        g = f(g)
        return np.ascontiguousarray(g.reshape(L, -1, 128).transpose(2, 0, 1))
    rt = np.zeros((128, 128), np.float32)
    for i in range(16):
        rt[i + 16, i] = -1.0
        rt[i, i + 16] = 1.0
    d = {
        "ones_f": np.ones((128, 128), np.float32),
        "rt_f": rt,
        "ident": np.eye(128, dtype=np.float32),
        "g_mix": cols(g_mix_norm), "g_mem": cols(g_mem_norm), "g_ffn": cols(g_ffn_norm),
        "g_go": cols(g_group_out),
        "g_fin": np.ascontiguousarray(np.broadcast_to(f(g_final)[None, :], (128, cfg.D))),
        "g_sgu": np.ascontiguousarray(np.broadcast_to(f(g_sgu).reshape(L, 512)[None], (128, L, 512))),
        "b_sp": np.ascontiguousarray(np.broadcast_to(f(b_spatial).reshape(L, 512)[None], (128, L, 512))),
        "w_spT": np.ascontiguousarray(f(w_spatial).transpose(3, 0, 1, 2)),
        "w_in_t": tile_w(f(w_in), cfg.D),
        "w_out_t": tile_w(f(w_out), 2048),
        "w_gu_t": tile_w(f(w_gate_up), cfg.D),
        "w_dn_t": tile_w(f(w_down), cfg.FF),
        "w_kv_t": tile_w(f(w_mem_kv), cfg.D),
    }
    return d


SAMPLE_CORES = (2, 4, 5, 6)


def kernel(x_prompt, x_sample, mem_prompt, mem_sample, g_mix_norm, w_in, g_sgu, w_spatial,
           b_spatial, g_mem_norm, w_mem_kv, g_group_out, w_out, g_ffn_norm, w_gate_up,
           w_down, g_final):
    cfg = Cfg(D=2048, FF=5632, T=4096, NSEG=1, DEPTH=4, HALO=True)
    nc = build(cfg)
    ci = const_inputs(cfg, g_mix_norm, w_in, g_sgu, w_spatial, b_spatial, g_mem_norm, w_mem_kv,
                      g_group_out, w_out, g_ffn_norm, w_gate_up, w_down, g_final)
    f = lambda a: np.ascontiguousarray(np.asarray(a, dtype=np.float32))
    x_prompt, x_sample, mem_prompt, mem_sample = f(x_prompt), f(x_sample), f(mem_prompt), f(mem_sample)
    T, D = cfg.T, cfg.D
    cos0, sin0 = rope_tables(np.arange(T))
    cos1, sin1 = rope_tables(T + np.arange(T))
    zx = np.zeros((T, D), np.float32)
    zm = np.zeros((1, 256, D), np.float32)
    in_maps = []
    for c in range(8):
        d = dict(ci)
        if c == 0:
            d.update({"x": np.ascontiguousarray(x_prompt[0, 0:T]), "mem": mem_prompt,
                      "cos": cos0, "sin": sin0, "masks": make_masks(False, right_nb=True)})
        elif c == 1:
            d.update({"x": np.ascontiguousarray(x_prompt[0, T:2 * T]), "mem": mem_prompt,
                      "cos": cos1, "sin": sin1, "masks": make_masks(False, left_nb=True)})
        elif c in SAMPLE_CORES:
            s = SAMPLE_CORES.index(c)
            d.update({"x": np.ascontiguousarray(x_sample[s]), "mem": np.ascontiguousarray(mem_sample[s:s + 1]),
                      "cos": cos0, "sin": sin0, "masks": make_masks(False)})
        else:
            d.update({"x": zx, "mem": zm, "cos": cos0, "sin": sin0, "masks": make_masks(False)})
        in_maps.append(d)
    res = run_bass_kernel_spmd(nc, in_maps, core_ids=list(range(8)))
    r = res.results
    y_prompt = np.concatenate([np.asarray(r[0]["y"], dtype=np.float32),
                               np.asarray(r[1]["y"], dtype=np.float32)], axis=0).reshape(1, 2 * T, D)
    y_sample = np.stack([np.asarray(r[c]["y"], dtype=np.float32) for c in SAMPLE_CORES], axis=0)
    return (y_prompt, y_sample)
```

```python
import numpy as np, math
import concourse.bass as bass
import concourse.mybir as mybir
from concourse.bass_utils import run_bass_kernel_spmd

F32 = mybir.dt.float32
BF16 = mybir.dt.bfloat16
ALU = mybir.AluOpType
AF = mybir.ActivationFunctionType

CENG = ('pe', 'act', 'dve', 'pool')
ENGS = ('pe', 'act', 'dve', 'pool', 'sp')
NSEM = 24


class Prog:
    def __init__(self, nc):
        self.nc = nc
        self.ops = {e: [] for e in ENGS}
        self.nins = {e: 0 for e in ENGS}
        self.insref = {e: [] for e in ENGS}
        self.clock = {e: [] for e in ENGS}
        self.known = {e: {c: -1 for c in CENG} for e in ENGS}
        self.known_dma = {e: set() for e in ENGS}
        self.dmas = []
        self.dma_count = {e: 0 for e in ENGS}
        self.dma_ids = {e: [] for e in ENGS}
        self.dma_ids['cc'] = []
        self.segs = {}

    @staticmethod
    def _region(ap):
        sp = str(ap.space)
        if 'SB' not in sp.upper() and 'PSUM' not in sp.upper():
            return None
        pat = ap.ap
        P = pat[0][0]
        off = ap.offset
        lo = off % P if P > 0 else off
        hi = lo + 1
        for st, cnt in pat[1:]:
            hi += abs(st) * (cnt - 1)
        es = 4 if ap.dtype == F32 else 2
        return (ap.tensor.name, lo * es, hi * es)

    def _access(self, reg, is_write, ev, deps):
        name, lo, hi = reg
        segs = self.segs.setdefault(name, [])
        out = []
        covered = []
        for s in segs:
            slo, shi, w, r = s
            if shi <= lo or slo >= hi:
                out.append(s)
                continue
            deps.extend(w)
            if is_write:
                deps.extend(r)
            if slo < lo:
                out.append((slo, lo, w, r))
            if shi > hi:
                out.append((hi, shi, w, r))
            ilo, ihi = max(slo, lo), min(shi, hi)
            if not is_write:
                nr = [x for x in r if not (x[0] == 'e' and ev[0] == 'e' and x[1] == ev[1])]
                nr.append(ev)
                out.append((ilo, ihi, w, nr))
                covered.append((ilo, ihi))
        if is_write:
            out.append((lo, hi, [ev], []))
        else:
            covered.sort()
            cur = lo
            for a, b in covered:
                if a > cur:
                    out.append((cur, a, [], [ev]))
                cur = max(cur, b)
            if cur < hi:
                out.append((cur, hi, [], [ev]))
        out.sort(key=lambda t: t[0])
        self.segs[name] = out

    def _add_waits(self, eng, deps):
        kn = self.known[eng]
        kd = self.known_dma[eng]
        best = {}
        dm = set()
        for d in deps:
            if d[0] == 'e':
                _, e2, i2 = d
                if e2 == 'pe' and eng == 'pe':
                    continue
                if kn[e2] >= i2:
                    continue
                if best.get(e2, -1) < i2:
                    best[e2] = i2
            else:
                if d[1] in kd:
                    continue
                dm.add(d[1])
        for e2, i2 in best.items():
            if kn[e2] >= i2:
                continue
            self.insref[e2][i2][2] = True
            self.ops[eng].append(['wait', ('e', e2, i2)])
            ck = self.clock[e2][i2]
            for j, c in enumerate(CENG):
                if ck[j] > kn[c]:
                    kn[c] = ck[j]
            if kn[e2] < i2:
                kn[e2] = i2
        for did in sorted(dm):
            self.ops[eng].append(['wait', ('d', did)])
            kd.add(did)
            ck = self.dmas[did][2]
            for j, c in enumerate(CENG):
                if ck[j] > kn[c]:
                    kn[c] = ck[j]

    def op(self, eng, fn, ins=(), outs=()):
        idx = self.nins[eng]
        ev = ('e', eng, idx)
        deps = []
        for ap in ins:
            r = self._region(ap)
            if r is not None:
                self._access(r, False, ev, deps)
        for ap in outs:
            r = self._region(ap)
            if r is not None:
                self._access(r, True, ev, deps)
        deps = [d for d in deps if d != ev]
        self._add_waits(eng, deps)
        ent = ['ins', fn, False, None]
        self.ops[eng].append(ent)
        self.insref[eng].append(ent)
        kn = self.known[eng]
        ck = tuple(idx if c == eng else kn[c] for c in CENG)
        self.clock[eng].append(ck)
        if eng == 'pe':
            kn['pe'] = idx - 1 if idx > 0 else -1
        self.nins[eng] = idx + 1
        return ev

    def dma(self, q, out, in_, fn=None, **kw):
        did = len(self.dmas)
        k = self.dma_count[q]
        ev = ('d', did)
        deps = []
        r = self._region(in_)
        if r is not None:
            self._access(r, False, ev, deps)
        r = self._region(out)
        if r is not None:
            self._access(r, True, ev, deps)
        if k >= NSEM:
            deps.append(('d', self.dma_ids[q][k - NSEM]))
        deps = [d for d in deps if d != ev]
        self._add_waits(q, deps)
        kn = self.known[q]
        ck = tuple(kn[c] for c in CENG)
        self.dmas.append((q, k, ck))
        self.dma_ids[q].append(did)
        self.dma_count[q] = k + 1

        if fn is None:
            def fn(e, out=out, in_=in_, kw=kw):
                return e.dma_start(out=out, in_=in_, **kw)
        ent = ['ins', fn, False, (q, k)]
        self.ops[q].append(ent)
        if q in CENG:
            pass
        return ev

    def coll(self, fn):
        did = len(self.dmas)
        k = len(self.dma_ids['cc'])
        kn = self.known['pool']
        ck = tuple(kn[c] for c in CENG)
        self.dmas.append(('cc', k, ck))
        self.dma_ids['cc'].append(did)
        self.ops['pool'].append(['ins', fn, False, ('cc', k)])
        return ('d', did)

    def barrier(self):
        last = {c: self.nins[c] - 1 for c in CENG}
        alld = list(range(len(self.dmas)))
        for e in ENGS:
            deps = [('e', c, last[c]) for c in CENG if last[c] >= 0 and c != e]
            if e in CENG and last[e] >= 0 and e != 'pe':
                deps.append(('e', e, last[e]))
            kd = self.known_dma[e]
            for q in ENGS + ('cc',):
                ids = self.dma_ids[q][-NSEM:]
                deps.extend(('d', i) for i in ids)
            self._add_waits(e, deps)
        for e in ENGS:
            self.known_dma[e] = set(alld[-4 * NSEM * len(ENGS):]) | self.known_dma[e]
        self.segs = {}

    def emit(self):
        nc = self.nc
        sems = {c: nc.alloc_semaphore("s_" + c) for c in CENG}
        dsems = {q: [nc.alloc_semaphore("d_%s_%d" % (q, i)) for i in range(NSEM)]
                 for q in ENGS if self.dma_count[q] > 0}
        ccsem = nc.alloc_semaphore("cc_sem") if self.dma_ids['cc'] else None
        cnt = {}
        for c in CENG:
            n = 0
            arr = []
            for ent in self.insref[c]:
                if ent[2]:
                    n += 1
                arr.append(n)
            cnt[c] = arr
        ops = self.ops
        dmas = self.dmas

        def run(engname, e):
            for ent in ops[engname]:
                if ent[0] == 'wait':
                    ev = ent[1]
                    if ev[0] == 'e':
                        e.wait_ge(sems[ev[1]], cnt[ev[1]][ev[2]])
                    else:
                        q, k, _ = dmas[ev[1]]
                        if q == 'cc':
                            e.wait_ge(ccsem, k + 1)
                        else:
                            e.wait_ge(dsems[q][k % NSEM], 16 * (k // NSEM + 1))
                else:
                    ins = ent[1](e)
                    if ent[3] is not None:
                        q, k = ent[3]
                        if q == 'cc':
                            ins.then_inc(ccsem, 1)
                        else:
                            ins.then_inc(dsems[q][k % NSEM], 16)
                    elif ent[2]:
                        ins.then_inc(sems[engname], 1)

        with nc.Block() as block:
            @block.sync
            def _(e):
                run('sp', e)

            @block.scalar
            def _(e):
                run('act', e)

            @block.vector
            def _(e):
                run('dve', e)

            @block.gpsimd
            def _(e):
                run('pool', e)

            @block.tensor
            def _(e):
                run('pe', e)

    def mm(self, out, lhsT, rhs, start=True, stop=True):
        return self.op('pe', lambda e: e.matmul(out, lhsT, rhs, start=start, stop=stop),
                       ins=(lhsT, rhs), outs=(out,))

    def transpose(self, out, in_, ident):
        return self.op('pe', lambda e: e.transpose(out, in_, ident), ins=(in_, ident), outs=(out,))

    def act(self, out, in_, func, scale=1.0, bias=0.0, accum_out=None, eng='act'):
        outs = (out,) if accum_out is None else (out, accum_out)
        insl = [in_]
        if not isinstance(scale, (int, float)):
            insl.append(scale)
        if not isinstance(bias, (int, float)):
            insl.append(bias)
        if accum_out is None:
            f = lambda e: e.activation(out, in_, func, bias=bias, scale=scale)
        else:
            f = lambda e: e.activation(out, in_, func, bias=bias, scale=scale, accum_out=accum_out)
        return self.op('act', f, ins=insl, outs=outs)

    def copy(self, eng, out, in_):
        if eng == 'act':
            return self.op('act', lambda e: e.copy(out, in_), ins=(in_,), outs=(out,))
        return self.op(eng, lambda e: e.tensor_copy(out, in_), ins=(in_,), outs=(out,))

    def tt(self, eng, out, in0, in1, op):
        return self.op(eng, lambda e: e.tensor_tensor(out, in0, in1, op), ins=(in0, in1), outs=(out,))

    def ts(self, eng, out, in0, s1, op0, s2=None, op1=None):
        insl = [in0]
        if not isinstance(s1, (int, float)):
            insl.append(s1)
        if s2 is not None and not isinstance(s2, (int, float)):
            insl.append(s2)
        if op1 is None:
            f = lambda e: e.tensor_scalar(out, in0, s1, None, op0)
        else:
            f = lambda e: e.tensor_scalar(out, in0, s1, s2, op0, op1)
        return self.op(eng, f, ins=insl, outs=(out,))

    def stt(self, eng, out, in0, scalar, in1, op0, op1):
        insl = [in0, in1]
        if not isinstance(scalar, (int, float)):
            insl.append(scalar)
        return self.op(eng, lambda e: e.scalar_tensor_tensor(out, in0, scalar, in1, op0, op1),
                       ins=insl, outs=(out,))

    def recip(self, out, in_):
        return self.op('dve', lambda e: e.reciprocal(out, in_), ins=(in_,), outs=(out,))

    def memset(self, eng, ap, val):
        return self.op(eng, lambda e: e.memset(ap, val), ins=(), outs=(ap,))


def sap(t, off, pat, np_=128):
    base = t[:]
    p0 = base.ap[0]
    return bass.AP(t, off, [[p0[0], np_]] + [list(x) for x in pat])


EPS = 1e-6
PAD = 1024
DILS = (1, 4, 16)
SCALE = 128 ** -0.5
G = 512


class Cfg:
    def __init__(self, D=2048, FF=5632, T=8192, NSEG=2, DEPTH=4, HALO=False):
        self.D, self.FF, self.T, self.NSEG, self.DEPTH = D, FF, T, NSEG, DEPTH
        self.HALO = HALO
        self.NCH = D // 128
        self.NFF = FF // 128
        self.NG = T // G
        self.TP = T + 2 * PAD
        self.TSEG = T // NSEG


class Arena:
    def __init__(self, nc, base, limit, prefix):
        self.nc, self.off, self.limit, self.prefix = nc, base, limit, prefix

    def t(self, name, shape, dtype):
        nb = 4 if dtype == F32 else 2
        n = 1
        for s in shape[1:]:
            n *= s
        size = (n * nb + 63) // 64 * 64
        h = self.nc.alloc_sbuf_tensor_at(self.prefix + name, list(shape), dtype, offset=self.off)
        self.off += size
        assert self.off <= self.limit, (self.prefix + name, self.off, self.limit)
        return h


class Streamer:
    def __init__(self, P, ring, ns, blocks, q='sp', pf=None):
        self.P, self.ring, self.ns, self.blocks, self.q = P, ring, ns, blocks, q
        self.pf = pf if pf is not None else ns - 4
        self.issued = 0

    def slot(self, i):
        return i % self.ns

    def get(self, i):
        upto = min(len(self.blocks), i + self.pf + 1)
        while self.issued < upto:
            j = self.issued
            src = self.blocks[j]
            n = src.shape[-1]
            dst = self.ring[:, j % self.ns, 0:n]
            self.P.dma(self.q, dst, src)
            self.issued += 1
        return i % self.ns


def dap(t_ap, off, pat):
    return bass.AP(t_ap.tensor, off, [list(x) for x in pat])


def build(cfg):
    nc = bass.Bass("TRN2", target_bir_lowering=False)
    P = Prog(nc)
    D, FF, T, NSEG, L = cfg.D, cfg.FF, cfg.T, cfg.NSEG, cfg.DEPTH
    NCH, NFF, NG, TP, TSEG = cfg.NCH, cfg.NFF, cfg.NG, cfg.TP, cfg.TSEG
    KW = NCH * 128

    def din(name, shape, dt=F32):
        return nc.dram_tensor(name, list(shape), dt, kind="ExternalInput").ap()

    def dscr(name, shape, dt):
        return nc.dram_tensor(name, list(shape), dt, kind="Internal").ap()

    x_d = din("x", [T, D])
    mem_d = din("mem", [NSEG, 256, D])
    cos_d = din("cos", [128, T])
    sin_d = din("sin", [128, T])
    masks_d = din("masks", [128, 4, 256])
    ones_d = din("ones_f", [128, 128])
    rt_d = din("rt_f", [128, 128])
    ident_d = din("ident", [128, 128])
    gmix_d = din("g_mix", [128, L, NCH])
    gmem_d = din("g_mem", [128, L, NCH])
    gffn_d = din("g_ffn", [128, L, NCH])
    ggo_d = din("g_go", [128, L, 16])
    gfin_d = din("g_fin", [128, D])
    gsgu_d = din("g_sgu", [128, L, 512])
    bsp_d = din("b_sp", [128, L, 512])
    wsp_d = din("w_spT", [128, L, 4, 128])
    win_d = din("w_in_t", [L, 36, 128, KW])
    wout_d = din("w_out_t", [L, NCH, 128, 2048])
    wgu_d = din("w_gu_t", [L, 2 * NFF, 128, KW])
    wdn_d = din("w_dn_t", [L, NCH, 128, NFF * 128])
    wkv_d = din("w_kv_t", [L, 8, 128, KW])
    y_d = nc.dram_tensor("y", [T, D], F32, kind="ExternalOutput").ap()

    wb_in = dscr("wb_in", [L, 36, 128, KW], BF16)
    wb_out = dscr("wb_out", [L, NCH, 128, 2048], BF16)
    wb_gu = dscr("wb_gu", [L, 2 * NFF, 128, KW], BF16)
    wb_dn = dscr("wb_dn", [L, NCH, 128, NFF * 128], BF16)
    wb_kv = dscr("wb_kv", [L, 8, 128, KW], BF16)
    xT_s = dscr("xT_s", [NCH, 128, T], F32)
    KT_s = dscr("KT_s", [8, 128, TP], BF16)
    QT_s = dscr("QT_s", [8, 128, T], BF16)
    V_s = dscr("V_s", [8, TP, 128], BF16)
    bT_s = dscr("bT_s", [8, 128, T], BF16)
    mA_s = dscr("mA_s", [4, 128, T], BF16)
    mC_s = dscr("mC_s", [4, 128, T], BF16)
    if cfg.HALO:
        snd = [dscr("snd%d" % i, [1024, 1024], BF16) for i in range(4)]
        rcv = [dscr("rcv%d" % i, [2 * 1024, 1024], BF16) for i in range(4)]

    SB_LIMIT = 229344
    CA = Arena(nc, 16512, SB_LIMIT, "c_")
    ones_b = CA.t("ones", [128, 128], BF16)
    rt_b = CA.t("rt", [128, 128], BF16)
    ident = CA.t("ident", [128, 128], F32)
    masks_b = CA.t("masks", [128, 4, 256], BF16)
    gmix = CA.t("gmix", [128, L, NCH], F32)
    gmem = CA.t("gmem", [128, L, NCH], F32)
    gffn = CA.t("gffn", [128, L, NCH], F32)
    ggo = CA.t("ggo", [128, L, 16], F32)
    memKT = CA.t("memKT", [128, NSEG, 4, 256], BF16)
    memV = CA.t("memV", [128, NSEG, 2, 512], BF16)
    gsgu = CA.t("gsgu", [128, 512], F32)
    bsp = CA.t("bsp", [128, 512], F32)
    wspT = CA.t("wspT", [128, 4, 128], BF16)
    ctmp = CA.t("ctmp", [128, 1024], F32)
    zpad = CA.t("zpad", [128, 1024], BF16)
    ident_b = CA.t("ident_b", [128, 128], BF16)
    negm = CA.t("negm", [128, 4, 256], BF16)
    C0 = CA.off

    ps = [nc.alloc_psum_tensor("ps%d" % i, [128, 512], F32) for i in range(8)]

    def cast_list(l):
        out = []
        for (dst, src, n) in ((wb_kv, wkv_d, 8), (wb_in, win_d, 36), (wb_out, wout_d, NCH),
                              (wb_gu, wgu_d, 2 * NFF), (wb_dn, wdn_d, NCH)):
            step = 4
            for o in range(0, n, step):
                e = min(n, o + step)
                out.append((dst[l, o:e], src[l, o:e]))
        return out

    def cast_blocks(l):
        out = []
        for (dst, src, n, width) in ((wb_kv, wkv_d, 8, KW), (wb_in, win_d, 36, KW), (wb_out, wout_d, NCH, 2048),
                                     (wb_gu, wgu_d, 2 * NFF, KW), (wb_dn, wdn_d, NCH, NFF * 128)):
            for o in range(n):
                for c0 in range(0, width, 1024):
                    c1 = min(width, c0 + 1024)
                    out.append((dst[l, o, :, c0:c1], src[l, o, :, c0:c1]))
        return out

    def init():
        P.dma('sp', ctmp[:, 0:128], ones_d)
        P.copy('dve', ones_b[:], ctmp[:, 0:128])
        P.dma('sp', ctmp[:, 128:256], rt_d)
        P.copy('dve', rt_b[:], ctmp[:, 128:256])
        P.dma('sp', ident[:], ident_d)
        P.dma('sp', ctmp[:, 0:1024].rearrange("p (a b) -> p a b", a=4), masks_d)
        P.copy('dve', masks_b[:], ctmp[:, 0:1024].rearrange("p (a b) -> p a b", a=4))
        P.ts('dve', negm[:], ctmp[:, 0:1024].rearrange("p (a b) -> p a b", a=4), 30000.0, ALU.mult, -30000.0, ALU.add)
        P.copy('dve', ident_b[:], ident[:])
        P.dma('sp', gmix[:], gmix_d)
        P.dma('sp', gmem[:], gmem_d)
        P.dma('sp', gffn[:], gffn_d)
        P.dma('sp', ggo[:], ggo_d)
        for (dst, src) in cast_list(0):
            P.dma('pool', dst, src)
        z = zpad
        P.memset('dve', z[:], 0.0)
        for h in range(8):
            P.dma('sp', KT_s[h, :, 0:PAD], z[:, 0:PAD])
            P.dma('sp', KT_s[h, :, PAD + T:TP], z[:, 0:PAD])
            for b0 in (0, PAD + T):
                P.dma('sp', V_s[h, b0:b0 + PAD, :].rearrange("(a p) d -> p a d", p=128),
                      z[:, 0:1024].rearrange("p (a d) -> p a d", d=128))

    def norm_fm(src, ncs, N, gcol, dst, nfeat, sq, ps_ss, srt, rstd, eng_mul=('dve',)):
        for c in range(ncs):
            P.act(sq[:, c, 0:N], src(c), AF.Square)
        for c in range(ncs):
            P.mm(ps_ss[:, 0:N], ones_b[:], sq[:, c, 0:N], start=(c == 0), stop=(c == ncs - 1))
        P.act(srt[:, 0:N], ps_ss[:, 0:N], AF.Sqrt, scale=1.0 / nfeat, bias=EPS)
        P.recip(rstd[:, 0:N], srt[:, 0:N])
        for c in range(ncs):
            P.stt('dve', dst(c), src(c), gcol(c), rstd[:, 0:N], ALU.mult, ALU.mult)

    def pass0():
        A = Arena(nc, C0, SB_LIMIT, "p0_")
        xt = [A.t("x%d" % i, [128, D], F32) for i in range(2)]
        stg = [A.t("s%d" % i, [128, NCH, 128], F32) for i in range(2)]
        nb = 0
        for tt in range(T // 128):
            xs = xt[tt % 2]
            sg = stg[tt % 2]
            P.dma('sp', xs[:], x_d[tt * 128:(tt + 1) * 128, :])
            for c0 in range(0, NCH, 4):
                bank = ps[nb % 4]
                nb += 1
                nn = min(4, NCH - c0)
                for c in range(c0, c0 + nn):
                    P.transpose(bank[:, (c - c0) * 128:(c - c0 + 1) * 128], xs[:, c * 128:(c + 1) * 128], ident[:])
                P.copy('act' if (nb % 2) else 'dve', sg[:, c0:c0 + nn, :],
                       bank[:, 0:nn * 128].rearrange("p (a b) -> p a b", b=128))
            P.dma('act', xT_s[:, :, tt * 128:(tt + 1) * 128].rearrange("c p t -> p c t"), sg[:])

    def passM(l):
        A = Arena(nc, C0, SB_LIMIT, "pm_")
        mt_ = A.t("mem", [128, 2, D], F32)
        memT = A.t("memT", [128, NCH, 256], F32)
        sq = A.t("sq", [128, max(NCH, 8), 256], BF16)
        memn = A.t("memn", [128, NCH, 256], BF16)
        srt = A.t("srt", [128, 256], F32)
        rstd = A.t("rstd", [128, 256], F32)
        wkv = A.t("wkv", [128, 8, KW], BF16)
        tmpf = A.t("tmpf", [128, 512], F32)
        P.dma('sp', wkv[:], wb_kv[l].rearrange("o p n -> p o n"))
        P.dma('sp', gsgu[:], gsgu_d[:, l, :])
        P.dma('sp', bsp[:], bsp_d[:, l, :])
        P.dma('sp', tmpf[:].rearrange("p (a b) -> p a b", a=4), wsp_d[:, l])
        P.copy('dve', wspT[:], tmpf[:].rearrange("p (a b) -> p a b", a=4))
        nb = 0
        for s in range(NSEG):
            P.dma('sp', mt_[:], mem_d[s].rearrange("(a p) d -> p a d", p=128))
            for a in range(2):
                for c0 in range(0, NCH, 4):
                    bank = ps[nb % 2]
                    nb += 1
                    nn = min(4, NCH - c0)
                    for c in range(c0, c0 + nn):
                        P.transpose(bank[:, (c - c0) * 128:(c - c0 + 1) * 128], mt_[:, a, c * 128:(c + 1) * 128], ident[:])
                    P.copy('dve', memT[:, c0:c0 + nn, a * 128:(a + 1) * 128],
                           bank[:, 0:nn * 128].rearrange("p (a b) -> p a b", b=128))
            norm_fm(lambda c: memT[:, c, :], NCH, 256, lambda c: gmem[:, l, c:c + 1],
                    lambda c: memn[:, c, :], D, sq, ps[2], srt, rstd)
            for hc in range(4):
                bank = ps[3 + hc % 2]
                for kc in range(NCH):
                    P.mm(bank[:, 0:256], wkv[:, hc, kc * 128:(kc + 1) * 128], memn[:, kc, :],
                         start=(kc == 0), stop=(kc == NCH - 1))
                P.copy('act', memKT[:, s, hc, :], bank[:, 0:256])
            for m in range(2):
                bank = ps[5 + m % 2]
                for kc in range(NCH):
                    P.mm(bank[:], memn[:, kc, m * 128:(m + 1) * 128],
                         sap(wkv, 4 * KW + kc * 128, [[KW, 4], [1, 128]]),
                         start=(kc == 0), stop=(kc == NCH - 1))
                P.copy('act', memV[:, s, m, :], bank[:])

    def passA(l):
        A = Arena(nc, C0, SB_LIMIT, "pa_")
        xT = A.t("xT", [128, NCH, G], F32)
        sq = A.t("sq", [128, max(NCH, 8), G], BF16)
        hT = A.t("hT", [128, NCH, G], BF16)
        srt = A.t("srt", [128, G], F32)
        rstd = A.t("rstd", [128, G], F32)
        NS = 8
        ring = A.t("ring", [128, NS, KW], BF16)
        cs = [A.t("cos%d" % i, [128, G], F32) for i in range(2)]
        sn = [A.t("sin%d" % i, [128, G], F32) for i in range(2)]
        uT = A.t("uT", [128, 4, G], F32)
        vtm = A.t("vtm", [128, 4, 512], F32)
        vtmp = A.t("vtmp", [128, 512], F32)
        vv = A.t("vv", [128, 4, 512], BF16)
        ssq = A.t("ssq", [128, 16], F32)
        ssr = A.t("ssr", [128, 16], F32)
        junk = A.t("junk", [128, 128], BF16)
        aT = A.t("aT", [128, 4, G], F32)
        atmp = [A.t("atmp%d" % i, [128, G], F32) for i in range(2)]
        amix = A.t("amix", [128, 4, G], BF16)
        qcT = A.t("qcT", [128, 4, G], BF16)
        PT = [A.t("PT%d" % i, [128, G], BF16) for i in range(2)]
        rden = A.t("rden", [128, G], F32)
        cT = A.t("cT", [128, 4, G], F32)
        cmix = A.t("cmix", [128, 4, G], BF16)
        zb = [A.t("zb%d" % i, [128, G], BF16) for i in range(2)]
        t1 = [A.t("t1_%d" % i, [128, G], F32) for i in range(2)]
        t2 = [A.t("t2_%d" % i, [128, G], F32) for i in range(2)]
        ro = [A.t("ro%d" % i, [128, G], BF16) for i in range(2)]
        vout = [A.t("vout%d" % i, [128, 1024], BF16) for i in range(4)]

        order = list(range(0, 4)) + list(range(4, 8)) + list(range(32, 36)) + list(range(8, 32))
        blocks = []
        for g in range(NG):
            for oc in order:
                blocks.append(wb_in[l, oc])
        st = Streamer(P, ring, NS, blocks, 'sp')
        psA = [ps[0], ps[1], ps[2]]
        ps_ss = ps[3]
        psm = [ps[4], ps[5]]
        ps_num, ps_den = ps[6], ps[7]
        cnt = {'a': 0, 'm': 0, 'z': 0}

        def nextA():
            b = psA[cnt['a'] % 3]
            cnt['a'] += 1
            return b

        def nextM():
            b = psm[cnt['m'] % 2]
            cnt['m'] += 1
            return b

        for g in range(NG):
            t0 = g * G
            seg = t0 // TSEG
            bi = g * 36
            cg, sg_ = cs[g % 2], sn[g % 2]
            if g == 0:
                P.dma('pool', xT[:], xT_s[:, :, t0:t0 + G].rearrange("c p t -> p c t"))
                P.dma('pool', cg[:], cos_d[:, t0:t0 + G])
                P.dma('pool', sg_[:], sin_d[:, t0:t0 + G])
            norm_fm(lambda c: xT[:, c, :], NCH, G, lambda c: gmix[:, l, c:c + 1],
                    lambda c: hT[:, c, :], D, sq, ps_ss, srt, rstd, eng_mul=('dve', 'pool'))
            if g + 1 < NG:
                t1n = t0 + G
                P.dma('pool', xT[:], xT_s[:, :, t1n:t1n + G].rearrange("c p t -> p c t"))
                P.dma('pool', cs[(g + 1) % 2][:], cos_d[:, t1n:t1n + G])
                P.dma('pool', sn[(g + 1) % 2][:], sin_d[:, t1n:t1n + G])

            def proj_fm(bidx):
                s_ = st.get(bidx)
                bank = nextA()
                for kc in range(NCH):
                    P.mm(bank[:], ring[:, s_, kc * 128:(kc + 1) * 128], hT[:, kc, :],
                         start=(kc == 0), stop=(kc == NCH - 1))
                return bank

            def proj_tm(bidx0, tt):
                s0 = st.get(bidx0)
                for j in range(1, 4):
                    st.get(bidx0 + j)
                assert s0 % 4 == 0
                bank = nextA()
                for kc in range(NCH):
                    P.mm(bank[:], hT[:, kc, tt * 128:(tt + 1) * 128],
                         sap(ring, s0 * KW + kc * 128, [[KW, 4], [1, 128]]),
                         start=(kc == 0), stop=(kc == NCH - 1))
                return bank

            for i in range(4):
                bank = proj_fm(bi + i)
                P.act(uT[:, i, :], bank[:], AF.Gelu_apprx_tanh)
            P.memset('pool', ssq[:], 0.0)
            for tt in range(4):
                bank = proj_tm(bi + 4, tt)
                P.act(vtm[:, tt, :], bank[:], AF.Gelu_apprx_tanh)
                for gi in range(4):
                    P.act(junk[:], vtm[:, tt, gi * 128:(gi + 1) * 128], AF.Square,
                          accum_out=ssq[:, tt * 4 + gi:tt * 4 + gi + 1])
            P.act(ssr[:], ssq[:], AF.Sqrt, scale=1.0 / 128, bias=EPS)
            P.recip(ssq[:], ssr[:])
            for tt in range(4):
                P.tt('dve', vtmp[:].rearrange("p (a b) -> p a b", a=4),
                     vtm[:, tt, :].rearrange("p (a b) -> p a b", a=4),
                     sap(ssq, tt * 4, [[1, 4], [0, 128]]), ALU.mult)
                P.tt('pool', vv[:, tt, :], vtmp[:], gsgu[:], ALU.mult)
            for i in range(4):
                bank = proj_fm(bi + 8 + i)
                P.copy('act', qcT[:, i, :], bank[:])
            for gi in range(4):
                bank = nextM()
                for tt in range(4):
                    P.mm(bank[:, tt * 128:(tt + 1) * 128], vv[:, tt, gi * 128:(gi + 1) * 128], wspT[:, gi, :])
                at = atmp[gi % 2]
                P.tt('dve', at[:].rearrange("p (a b) -> p a b", a=4),
                     bank[:].rearrange("p (a b) -> p a b", a=4),
                     sap(bsp, gi * 128, [[0, 4], [1, 128]]), ALU.add)
                P.tt('pool', aT[:, gi, :], at[:], uT[:, gi, :], ALU.mult)
            norm_fm(lambda c: aT[:, c, :], 4, G, lambda c: ggo[:, l, c:c + 1],
                    lambda c: amix[:, c, :], 512, sq, ps_ss, srt, rstd, eng_mul=('pool',))
            P.dma('pool', mA_s[:, :, t0:t0 + G].rearrange("c p t -> p c t"), amix[:])
            def rope_a(i):
                bank = proj_fm(bi + 12 + i)
                k = cnt['z'] % 2
                cnt['z'] += 1
                P.copy('act', zb[k][:], bank[:])
                return (i, k)

            def rope_b(stt_):
                i, k = stt_
                h = i % 8
                isk = i >= 8
                z, o, a1, a2 = zb[k], ro[k], t1[k], t2[k]
                b2 = nextM()
                P.mm(b2[:], rt_b[:], z[:])
                P.tt('dve', a1[:], z[:], cg[:], ALU.mult)
                P.tt('dve', a2[:], b2[:], sg_[:], ALU.mult)
                P.tt('pool', o[:], a1[:], a2[:], ALU.add)
                if isk:
                    P.dma('pool', KT_s[h, :, PAD + t0:PAD + t0 + G], o[:])
                else:
                    P.dma('pool', QT_s[h, :, t0:t0 + G], o[:])

            def cross_a(hc):
                for m_ in range(2):
                    bank = nextM()
                    P.mm(bank[:], memKT[:, seg, hc, m_ * 128:(m_ + 1) * 128], qcT[:, hc, :])
                    P.act(PT[m_][:], bank[:], AF.Exp, scale=SCALE)

            def cross_b(hc):
                for m_ in range(2):
                    P.mm(ps_num[:], memV[:, seg, m_, hc * 128:(hc + 1) * 128], PT[m_][:], start=(m_ == 0), stop=(m_ == 1))
                for m_ in range(2):
                    P.mm(ps_den[:], ones_b[:], PT[m_][:], start=(m_ == 0), stop=(m_ == 1))
                P.recip(rden[:], ps_den[:])
                P.tt('dve', cT[:, hc, :], ps_num[:], rden[:], ALU.mult)

            pend = None
            for i in range(16):
                cur = rope_a(i)
                if pend is not None:
                    rope_b(pend)
                pend = cur
                if i in (1, 3, 5, 7):
                    cross_a((i - 1) // 2)
                if i in (2, 4, 6, 8):
                    cross_b((i - 2) // 2)
                if i == 10:
                    norm_fm(lambda c: cT[:, c, :], 4, G, lambda c: ggo[:, l, 12 + c:13 + c],
                            lambda c: cmix[:, c, :], 512, sq, ps_ss, srt, rstd, eng_mul=('pool',))
                    P.dma('pool', mC_s[:, :, t0:t0 + G].rearrange("c p t -> p c t"), cmix[:])
            rope_b(pend)
            for blk in range(2):
                for tt in range(4):
                    bank = proj_tm(bi + 28 + blk * 4, tt)
                    P.copy('act' if tt % 2 == 0 else 'dve', vout[tt][:, blk * 512:(blk + 1) * 512], bank[:])
            for tt in range(4):
                tk = PAD + t0 + tt * 128
                P.dma('pool', V_s[:, tk:tk + 128, :].rearrange("h p d -> p h d"),
                      vout[tt][:].rearrange("p (h d) -> p h d", h=8))

    def passB(l):
        A = Arena(nc, C0, SB_LIMIT, "pb_")
        KThs = [A.t("KTh%d" % i, [128, TP], BF16) for i in range(2)]
        QThs = [A.t("QTh%d" % i, [128, T], BF16) for i in range(2)]
        accns = [A.t("accn%d" % i, [128, T], F32) for i in range(2)]
        accds = [A.t("accd%d" % i, [128, T], F32) for i in range(2)]
        bTh = A.t("bTh", [128, T], BF16)
        NTV = 17
        NVS = 4
        vt = A.t("vt", [128, NVS, NTV * 128], BF16)
        psN = [ps[2], ps[3]]
        psD = [ps[4], ps[5]]
        cnt = {'s': 0, 'v': 0, 'p': 0}
        jobs = []
        for d in DILS:
            Lc = T // d
            nq = Lc // 128
            for r in range(d):
                i = 0
                while i <= nq:
                    n = min(NTV, nq + 1 - i)
                    jobs.append((d, r, i, n, nq))
                    i += n
        LA = 3
        NPT = 6
        PTt = [A.t("PTx%d" % i, [128, 256], BF16) for i in range(NPT)]
        psS = [ps[0], ps[1], ps[6], ps[7]]
        tiles = []
        for jn, (d, r, i0, n, nq) in enumerate(jobs):
            for ii in range(n):
                tiles.append((jn, d, r, i0 + ii, ii, nq))
        ntl = len(tiles)

        def vload(h, jn):
            d, r, i0, n, nq = jobs[jn]
            vs = jn % NVS
            off = (PAD + r + d * (128 * i0 - 64)) * 128
            src = dap(V_s[h], V_s[h].offset + off, [[d * 128, 128], [d * 128 * 128, n], [1, 128]])
            P.dma('sp', vt[:, vs, 0:n * 128].rearrange("p (a b) -> p a b", b=128), src)

        P.dma('sp', KThs[0][:], KT_s[0])
        P.dma('sp', QThs[0][:], QT_s[0])
        for h in range(8):
            KTh, QTh, accn, accd = KThs[h % 2], QThs[h % 2], accns[h % 2], accds[h % 2]
            if h + 1 < 8:
                P.dma('sp', KThs[(h + 1) % 2][:], KT_s[h + 1])
                P.dma('sp', QThs[(h + 1) % 2][:], QT_s[h + 1])
            vload(h, 0)
            for idx in range(ntl + LA):
                if idx < ntl:
                    jn, d, r, i, ii, nq = tiles[idx]
                    if ii == 0 and jn + 1 < len(jobs):
                        vload(h, jn + 1)
                    c0 = 128 if i == 0 else 0
                    c1 = 128 if i == nq else 256
                    kind = 0
                    if i == 0:
                        kind = 1
                    elif i == nq:
                        kind = 2
                    elif NSEG == 2 and i == nq // 2:
                        kind = 3
                    bS = psS[idx % 4]
                    pt = PTt[idx % NPT]
                    P.mm(bS[:, c0:c1], sap(KTh, PAD + r + d * (128 * i - 64), [[d, 128]]),
                         sap(QTh, r + d * (128 * (i - 1) + c0), [[d, c1 - c0]]), start=True, stop=False)
                    P.mm(bS[:, c0:c1], ident_b[:], negm[:, kind, c0:c1], start=False, stop=True)
                    P.act(pt[:, c0:c1], bS[:, c0:c1], AF.Exp, scale=SCALE)
                k2 = idx - LA
                if k2 >= 0:
                    jn, d, r, i, ii, nq = tiles[k2]
                    vs = jn % NVS
                    pt = PTt[k2 % NPT]
                    for hh in (0, 1):
                        j = i - 1 + hh
                        if j < 0 or j >= nq:
                            continue
                        P.mm(psN[j % 2][:, 0:128], vt[:, vs, ii * 128:(ii + 1) * 128], pt[:, hh * 128:(hh + 1) * 128],
                             start=(hh == 1), stop=(hh == 0))
                        P.mm(psD[j % 2][:, 0:128], ones_b[:], pt[:, hh * 128:(hh + 1) * 128],
                             start=(hh == 1), stop=(hh == 0))
                        if hh == 0:
                            tok = r + d * 128 * j
                            an = sap(accn, tok, [[d, 128]])
                            ad = sap(accd, tok, [[d, 128]])
                            if d == 1:
                                P.copy('dve', an, psN[j % 2][:, 0:128])
                                P.copy('act', ad, psD[j % 2][:, 0:128])
                            else:
                                P.tt('dve', an, psN[j % 2][:, 0:128], an, ALU.add)
                                P.tt('dve', ad, psD[j % 2][:, 0:128], ad, ALU.add)
            CH = 2048
            for c in range(0, T, CH):
                P.recip(accd[:, c:c + CH], accd[:, c:c + CH])
                P.tt('pool', bTh[:, c:c + CH], accn[:, c:c + CH], accd[:, c:c + CH], ALU.mult)
                P.dma('pool', bT_s[h, :, c:c + CH], bTh[:, c:c + CH])

    def passC(l):
        last = (l == L - 1)
        A = Arena(nc, C0, SB_LIMIT, "pc_")
        xT = A.t("xT", [128, NCH, G], F32)
        mixT = A.t("mixT", [128, 16, G], BF16)
        sq = A.t("sq", [128, max(NCH, 8), G], BF16)
        hT = A.t("hT", [128, NCH, G], BF16)
        srt = A.t("srt", [128, G], F32)
        rstd = A.t("rstd", [128, G], F32)
        actT = A.t("actT", [128, NFF, G], BF16)
        NS = 6 if last else 8
        ring = A.t("ring", [128, NS, 2048], BF16)
        sgt = [A.t("sg%d" % i, [128, G], F32) for i in range(2)]
        if last:
            xtm = [A.t("xtm%d" % i, [128, D], F32) for i in range(1)]
            ytm = [A.t("ytm%d" % i, [128, D], F32) for i in range(1)]
            gfin = A.t("gfin", [128, D], F32)
            ssq = A.t("ssq", [128, 8], F32)
            ssr = A.t("ssr", [128, 8], F32)
            P.dma('act', gfin[:], gfin_d)
        nparts = (NFF + 15) // 16
        blocks = []
        for g in range(NG):
            for oc in range(NCH):
                blocks.append(wb_out[l, oc])
            for j in range(NFF):
                blocks.append(wb_gu[l, j])
                blocks.append(wb_gu[l, NFF + j])
            for oc in range(NCH):
                for pt_ in range(nparts):
                    k0 = pt_ * 16
                    k1 = min(NFF, k0 + 16)
                    blocks.append(wb_dn[l, oc, :, k0 * 128:k1 * 128])
        st = Streamer(P, ring, NS, blocks, 'sp', pf=NS - 3)
        nblk = NCH + 2 * NFF + NCH * nparts
        psA = [ps[0], ps[1], ps[2], ps[3], ps[4], ps[5]]
        ps_ss = ps[6]
        psT = ps[7]
        cnt = {'a': 0, 'f': 0}

        def nextA():
            b = psA[cnt['a'] % 6]
            cnt['a'] += 1
            return b

        bgc = cast_blocks(l + 1) if l + 1 < L else []
        per = (len(bgc) + NG - 1) // NG if bgc else 0
        if bgc:
            stf = [A.t("stf%d" % i, [128, 1024], F32) for i in range(2)]
            stb = [A.t("stb%d" % i, [128, 1024], BF16) for i in range(2)]
        ccnt = {'k': 0}
        for g in range(NG):
            t0 = g * G
            bi = g * nblk
            blk = bgc[g * per:(g + 1) * per]
            for bi_, (dst_, src_) in enumerate(blk):
                k = ccnt['k']
                ccnt['k'] += 1
                n_ = src_.shape[-1]
                if bi_ == 0:
                    P.dma('pool', stf[k % 2][:, 0:n_], src_)
                if bi_ + 1 < len(blk):
                    nsrc = blk[bi_ + 1][1]
                    P.dma('pool', stf[(k + 1) % 2][:, 0:nsrc.shape[-1]], nsrc)
                P.copy('pool', stb[k % 2][:, 0:n_], stf[k % 2][:, 0:n_])
                P.dma('pool', dst_, stb[k % 2][:, 0:n_])
            P.dma('act', xT[:], xT_s[:, :, t0:t0 + G].rearrange("c p t -> p c t"))
            P.dma('act', mixT[:, 0:4, :], mA_s[:, :, t0:t0 + G].rearrange("c p t -> p c t"))
            P.dma('act', mixT[:, 4:12, :], bT_s[:, :, t0:t0 + G].rearrange("c p t -> p c t"))
            P.dma('act', mixT[:, 12:16, :], mC_s[:, :, t0:t0 + G].rearrange("c p t -> p c t"))
            norm_fm(lambda c: mixT[:, 4 + c, :], 8, G, lambda c: ggo[:, l, 4 + c:5 + c],
                    lambda c: mixT[:, 4 + c, :], 1024, sq, ps_ss, srt, rstd, eng_mul=('dve', 'pool'))
            for oc in range(NCH):
                s_ = st.get(bi + oc)
                bank = nextA()
                for kc in range(16):
                    P.mm(bank[:], ring[:, s_, kc * 128:(kc + 1) * 128], mixT[:, kc, :],
                         start=(kc == 0), stop=(kc == 15))
                P.tt('dve', xT[:, oc, :], bank[:], xT[:, oc, :], ALU.add)
            norm_fm(lambda c: xT[:, c, :], NCH, G, lambda c: gffn[:, l, c:c + 1],
                    lambda c: hT[:, c, :], D, sq, ps_ss, srt, rstd, eng_mul=('dve', 'pool'))
            b0 = bi + NCH
            for j in range(NFF):
                sg_ = st.get(b0 + 2 * j)
                su_ = st.get(b0 + 2 * j + 1)
                bg = nextA()
                for kc in range(NCH):
                    P.mm(bg[:], ring[:, sg_, kc * 128:(kc + 1) * 128], hT[:, kc, :],
                         start=(kc == 0), stop=(kc == NCH - 1))
                bu = nextA()
                for kc in range(NCH):
                    P.mm(bu[:], ring[:, su_, kc * 128:(kc + 1) * 128], hT[:, kc, :],
                         start=(kc == 0), stop=(kc == NCH - 1))
                sgb = sgt[j % 2]
                P.act(sgb[:], bg[:], AF.Silu)
                P.tt('dve', actT[:, j, :], bu[:], sgb[:], ALU.mult)
            b1 = b0 + 2 * NFF
            for oc in range(NCH):
                bank = nextA()
                for pt_ in range(nparts):
                    s_ = st.get(b1 + oc * nparts + pt_)
                    k0 = pt_ * 16
                    k1 = min(NFF, k0 + 16)
                    for kc in range(k0, k1):
                        P.mm(bank[:], ring[:, s_, (kc - k0) * 128:(kc - k0 + 1) * 128], actT[:, kc, :],
                             start=(kc == 0), stop=(kc == NFF - 1))
                P.tt('dve', xT[:, oc, :], bank[:], xT[:, oc, :], ALU.add)
            if not last:
                P.dma('act', xT_s[:, :, t0:t0 + G].rearrange("c p t -> p c t"), xT[:])
            else:
                for tt in range(4):
                    xm = xtm[0]
                    ym = ytm[0]
                    for c0 in range(0, NCH, 4):
                        nn = min(4, NCH - c0)
                        for c in range(c0, c0 + nn):
                            P.transpose(psT[:, (c - c0) * 128:(c - c0 + 1) * 128], xT[:, c, tt * 128:(tt + 1) * 128], ident[:])
                        P.copy('act' if (c0 // 4) % 2 == 0 else 'dve', xm[:, c0 * 128:(c0 + nn) * 128], psT[:, 0:nn * 128])
                    P.memset('pool', ssq[:, 0:1], 0.0)
                    P.act(sap(sq, 0, [[1, D]]), xm[:], AF.Square, accum_out=ssq[:, 0:1])
                    P.act(ssr[:, 0:1], ssq[:, 0:1], AF.Sqrt, scale=1.0 / D, bias=EPS)
                    P.recip(ssq[:, 1:2], ssr[:, 0:1])
                    P.stt('dve', ym[:], xm[:], ssq[:, 1:2], gfin[:], ALU.mult, ALU.mult)
                    P.dma('act', y_d[t0 + tt * 128:t0 + (tt + 1) * 128, :], ym[:])

    def kview(buf, r0):
        return buf[r0:r0 + 1024, :].rearrange("(h p) t -> h p t", h=8)

    def vview(buf, r0):
        return buf[r0:r0 + 1024, :].rearrange("(h a) (b d) -> h (a b) d", h=8, b=8)

    def exchange(l):
        P.dma('sp', kview(snd[0], 0), KT_s[:, :, PAD:PAD + 1024])
        P.dma('sp', vview(snd[1], 0), V_s[:, PAD:PAD + 1024, :])
        P.dma('sp', kview(snd[2], 0), KT_s[:, :, PAD + T - 1024:PAD + T])
        P.dma('sp', vview(snd[3], 0), V_s[:, PAD + T - 1024:PAD + T, :])
        P.barrier()
        groups = [[0, 1], [2, 3], [4, 5], [6, 7]]
        for i in range(4):
            P.coll(lambda e, i=i: e.collective_compute(
                "AllGather", ALU.bypass, replica_groups=groups, ins=[snd[i]], outs=[rcv[i]]))
            P.barrier()
        P.dma('sp', KT_s[:, :, 0:PAD], kview(rcv[2], 0))
        P.dma('sp', V_s[:, 0:PAD, :], vview(rcv[3], 0))
        P.dma('sp', KT_s[:, :, PAD + T:TP], kview(rcv[0], 1024))
        P.dma('sp', V_s[:, PAD + T:TP, :], vview(rcv[1], 1024))

    init()
    pass0()
    P.barrier()
    for l in range(L):
        passM(l)
        P.barrier()
        passA(l)
        P.barrier()
        if cfg.HALO:
            exchange(l)
            P.barrier()
        passB(l)
        P.barrier()
        passC(l)
        P.barrier()
    P.emit()
    return nc


def rope_tables(pos):
    inv = (np.float32(500000.0) ** (-np.arange(0, 32, 2, dtype=np.float32) / np.float32(32))).astype(np.float32)
    ang = pos.astype(np.float32)[:, None] * inv[None, :]
    c = np.cos(ang).astype(np.float32).T
    s = np.sin(ang).astype(np.float32).T
    T = pos.shape[0]
    cos = np.ones((128, T), np.float32)
    sin = np.zeros((128, T), np.float32)
    cos[0:16] = c
    cos[16:32] = c
    sin[0:16] = s
    sin[16:32] = s
    return cos, sin


def make_masks(split_mid, left_nb=False, right_nb=False):
    kk = np.arange(128)[:, None]
    c = np.arange(256)[None, :]
    band = ((c >= kk) & (c <= kk + 128))
    m = np.zeros((128, 4, 256), np.float32)
    m[:, 0] = band
    m[:, 1] = band if left_nb else (band & (kk >= 64))
    m[:, 2] = band if right_nb else (band & (kk < 64))
    if split_mid:
        m[:, 3] = band & ((kk < 64) == (c < 128))
    else:
        m[:, 3] = band
    return m


def tile_w(w, kdim):
    Lw, K, N = w.shape
    a = w.reshape(Lw, K // 128, 128, N // 128, 128)
    return np.ascontiguousarray(a.transpose(0, 3, 2, 1, 4)).reshape(Lw, N // 128, 128, K)


def const_inputs(cfg, g_mix_norm, w_in, g_sgu, w_spatial, b_spatial, g_mem_norm, w_mem_kv,
                 g_group_out, w_out, g_ffn_norm, w_gate_up, w_down, g_final):
    L = cfg.DEPTH
    f = lambda a: np.ascontiguousarray(np.asarray(a, dtype=np.float32))

    def cols(g):
        g = f(g)
        return np.ascontiguousarray(g.reshape(L, -1, 128).transpose(2, 0, 1))
    rt = np.zeros((128, 128), np.float32)
    for i in range(16):
        rt[i + 16, i] = -1.0
        rt[i, i + 16] = 1.0
    d = {
        "ones_f": np.ones((128, 128), np.float32),
        "rt_f": rt,
        "ident": np.eye(128, dtype=np.float32),
        "g_mix": cols(g_mix_norm), "g_mem": cols(g_mem_norm), "g_ffn": cols(g_ffn_norm),
        "g_go": cols(g_group_out),
        "g_fin": np.ascontiguousarray(np.broadcast_to(f(g_final)[None, :], (128, cfg.D))),
        "g_sgu": np.ascontiguousarray(np.broadcast_to(f(g_sgu).reshape(L, 512)[None], (128, L, 512))),
        "b_sp": np.ascontiguousarray(np.broadcast_to(f(b_spatial).reshape(L, 512)[None], (128, L, 512))),
        "w_spT": np.ascontiguousarray(f(w_spatial).transpose(3, 0, 1, 2)),
        "w_in_t": tile_w(f(w_in), cfg.D),
        "w_out_t": tile_w(f(w_out), 2048),
        "w_gu_t": tile_w(f(w_gate_up), cfg.D),
        "w_dn_t": tile_w(f(w_down), cfg.FF),
        "w_kv_t": tile_w(f(w_mem_kv), cfg.D),
    }
    return d


SAMPLE_CORES = (2, 4, 5, 6)


def kernel(x_prompt, x_sample, mem_prompt, mem_sample, g_mix_norm, w_in, g_sgu, w_spatial,
           b_spatial, g_mem_norm, w_mem_kv, g_group_out, w_out, g_ffn_norm, w_gate_up,
           w_down, g_final):
    cfg = Cfg(D=2048, FF=5632, T=4096, NSEG=1, DEPTH=4, HALO=True)
    nc = build(cfg)
    ci = const_inputs(cfg, g_mix_norm, w_in, g_sgu, w_spatial, b_spatial, g_mem_norm, w_mem_kv,
                      g_group_out, w_out, g_ffn_norm, w_gate_up, w_down, g_final)
    f = lambda a: np.ascontiguousarray(np.asarray(a, dtype=np.float32))
    x_prompt, x_sample, mem_prompt, mem_sample = f(x_prompt), f(x_sample), f(mem_prompt), f(mem_sample)
    T, D = cfg.T, cfg.D
    cos0, sin0 = rope_tables(np.arange(T))
    cos1, sin1 = rope_tables(T + np.arange(T))
    zx = np.zeros((T, D), np.float32)
    zm = np.zeros((1, 256, D), np.float32)
    in_maps = []
    for c in range(8):
        d = dict(ci)
        if c == 0:
            d.update({"x": np.ascontiguousarray(x_prompt[0, 0:T]), "mem": mem_prompt,
                      "cos": cos0, "sin": sin0, "masks": make_masks(False, right_nb=True)})
        elif c == 1:
            d.update({"x": np.ascontiguousarray(x_prompt[0, T:2 * T]), "mem": mem_prompt,
                      "cos": cos1, "sin": sin1, "masks": make_masks(False, left_nb=True)})
        elif c in SAMPLE_CORES:
            s = SAMPLE_CORES.index(c)
            d.update({"x": np.ascontiguousarray(x_sample[s]), "mem": np.ascontiguousarray(mem_sample[s:s + 1]),
                      "cos": cos0, "sin": sin0, "masks": make_masks(False)})
        else:
            d.update({"x": zx, "mem": zm, "cos": cos0, "sin": sin0, "masks": make_masks(False)})
        in_maps.append(d)
    res = run_bass_kernel_spmd(nc, in_maps, core_ids=list(range(8)))
    r = res.results
    y_prompt = np.concatenate([np.asarray(r[0]["y"], dtype=np.float32),
                               np.asarray(r[1]["y"], dtype=np.float32)], axis=0).reshape(1, 2 * T, D)
    y_sample = np.stack([np.asarray(r[c]["y"], dtype=np.float32) for c in SAMPLE_CORES], axis=0)
    return (y_prompt, y_sample)
```

```python
import numpy as np, math
import concourse.bass as bass
import concourse.mybir as mybir
from concourse.bass_utils import run_bass_kernel_spmd

F32 = mybir.dt.float32
BF16 = mybir.dt.bfloat16
ALU = mybir.AluOpType
AF = mybir.ActivationFunctionType

CENG = ('pe', 'act', 'dve', 'pool')
ENGS = ('pe', 'act', 'dve', 'pool', 'sp')
NSEM = 24


class Prog:
    def __init__(self, nc):
        self.nc = nc
        self.ops = {e: [] for e in ENGS}
        self.nins = {e: 0 for e in ENGS}
        self.insref = {e: [] for e in ENGS}
        self.clock = {e: [] for e in ENGS}
        self.known = {e: {c: -1 for c in CENG} for e in ENGS}
        self.known_dma = {e: set() for e in ENGS}
        self.dmas = []
        self.dma_count = {e: 0 for e in ENGS}
        self.dma_ids = {e: [] for e in ENGS}
        self.dma_ids['cc'] = []
        self.segs = {}

    @staticmethod
    def _region(ap):
        sp = str(ap.space)
        if 'SB' not in sp.upper() and 'PSUM' not in sp.upper():
            return None
        pat = ap.ap
        P = pat[0][0]
        off = ap.offset
        lo = off % P if P > 0 else off
        hi = lo + 1
        for st, cnt in pat[1:]:
            hi += abs(st) * (cnt - 1)
        es = 4 if ap.dtype == F32 else 2
        return (ap.tensor.name, lo * es, hi * es)

    def _access(self, reg, is_write, ev, deps):
        name, lo, hi = reg
        segs = self.segs.setdefault(name, [])
        out = []
        covered = []
        for s in segs:
            slo, shi, w, r = s
            if shi <= lo or slo >= hi:
                out.append(s)
                continue
            deps.extend(w)
            if is_write:
                deps.extend(r)
            if slo < lo:
                out.append((slo, lo, w, r))
            if shi > hi:
                out.append((hi, shi, w, r))
            ilo, ihi = max(slo, lo), min(shi, hi)
            if not is_write:
                nr = [x for x in r if not (x[0] == 'e' and ev[0] == 'e' and x[1] == ev[1])]
                nr.append(ev)
                out.append((ilo, ihi, w, nr))
                covered.append((ilo, ihi))
        if is_write:
            out.append((lo, hi, [ev], []))
        else:
            covered.sort()
            cur = lo
            for a, b in covered:
                if a > cur:
                    out.append((cur, a, [], [ev]))
                cur = max(cur, b)
            if cur < hi:
                out.append((cur, hi, [], [ev]))
        out.sort(key=lambda t: t[0])
        self.segs[name] = out

    def _add_waits(self, eng, deps):
        kn = self.known[eng]
        kd = self.known_dma[eng]
        best = {}
        dm = set()
        for d in deps:
            if d[0] == 'e':
                _, e2, i2 = d
                if e2 == 'pe' and eng == 'pe':
                    continue
                if kn[e2] >= i2:
                    continue
                if best.get(e2, -1) < i2:
                    best[e2] = i2
            else:
                if d[1] in kd:
                    continue
                dm.add(d[1])
        for e2, i2 in best.items():
            if kn[e2] >= i2:
                continue
            self.insref[e2][i2][2] = True
            self.ops[eng].append(['wait', ('e', e2, i2)])
            ck = self.clock[e2][i2]
            for j, c in enumerate(CENG):
                if ck[j] > kn[c]:
                    kn[c] = ck[j]
            if kn[e2] < i2:
                kn[e2] = i2
        for did in sorted(dm):
            self.ops[eng].append(['wait', ('d', did)])
            kd.add(did)
            ck = self.dmas[did][2]
            for j, c in enumerate(CENG):
                if ck[j] > kn[c]:
                    kn[c] = ck[j]

    def op(self, eng, fn, ins=(), outs=()):
        idx = self.nins[eng]
        ev = ('e', eng, idx)
        deps = []
        for ap in ins:
            r = self._region(ap)
            if r is not None:
                self._access(r, False, ev, deps)
        for ap in outs:
            r = self._region(ap)
            if r is not None:
                self._access(r, True, ev, deps)
        deps = [d for d in deps if d != ev]
        self._add_waits(eng, deps)
        ent = ['ins', fn, False, None]
        self.ops[eng].append(ent)
        self.insref[eng].append(ent)
        kn = self.known[eng]
        ck = tuple(idx if c == eng else kn[c] for c in CENG)
        self.clock[eng].append(ck)
        if eng == 'pe':
            kn['pe'] = idx - 1 if idx > 0 else -1
        self.nins[eng] = idx + 1
        return ev

    def dma(self, q, out, in_, fn=None, **kw):
        did = len(self.dmas)
        k = self.dma_count[q]
        ev = ('d', did)
        deps = []
        r = self._region(in_)
        if r is not None:
            self._access(r, False, ev, deps)
        r = self._region(out)
        if r is not None:
            self._access(r, True, ev, deps)
        if k >= NSEM:
            deps.append(('d', self.dma_ids[q][k - NSEM]))
        deps = [d for d in deps if d != ev]
        self._add_waits(q, deps)
        kn = self.known[q]
        ck = tuple(kn[c] for c in CENG)
        self.dmas.append((q, k, ck))
        self.dma_ids[q].append(did)
        self.dma_count[q] = k + 1

        if fn is None:
            def fn(e, out=out, in_=in_, kw=kw):
                return e.dma_start(out=out, in_=in_, **kw)
        ent = ['ins', fn, False, (q, k)]
        self.ops[q].append(ent)
        if q in CENG:
            pass
        return ev

    def coll(self, fn):
        did = len(self.dmas)
        k = len(self.dma_ids['cc'])
        kn = self.known['pool']
        ck = tuple(kn[c] for c in CENG)
        self.dmas.append(('cc', k, ck))
        self.dma_ids['cc'].append(did)
        self.ops['pool'].append(['ins', fn, False, ('cc', k)])
        return ('d', did)

    def barrier(self):
        last = {c: self.nins[c] - 1 for c in CENG}
        alld = list(range(len(self.dmas)))
        for e in ENGS:
            deps = [('e', c, last[c]) for c in CENG if last[c] >= 0 and c != e]
            if e in CENG and last[e] >= 0 and e != 'pe':
                deps.append(('e', e, last[e]))
            kd = self.known_dma[e]
            for q in ENGS + ('cc',):
                ids = self.dma_ids[q][-NSEM:]
                deps.extend(('d', i) for i in ids)
            self._add_waits(e, deps)
        for e in ENGS:
            self.known_dma[e] = set(alld[-4 * NSEM * len(ENGS):]) | self.known_dma[e]
        self.segs = {}

    def emit(self):
        nc = self.nc
        sems = {c: nc.alloc_semaphore("s_" + c) for c in CENG}
        dsems = {q: [nc.alloc_semaphore("d_%s_%d" % (q, i)) for i in range(NSEM)]
                 for q in ENGS if self.dma_count[q] > 0}
        ccsem = nc.alloc_semaphore("cc_sem") if self.dma_ids['cc'] else None
        cnt = {}
        for c in CENG:
            n = 0
            arr = []
            for ent in self.insref[c]:
                if ent[2]:
                    n += 1
                arr.append(n)
            cnt[c] = arr
        ops = self.ops
        dmas = self.dmas

        def run(engname, e):
            for ent in ops[engname]:
                if ent[0] == 'wait':
                    ev = ent[1]
                    if ev[0] == 'e':
                        e.wait_ge(sems[ev[1]], cnt[ev[1]][ev[2]])
                    else:
                        q, k, _ = dmas[ev[1]]
                        if q == 'cc':
                            e.wait_ge(ccsem, k + 1)
                        else:
                            e.wait_ge(dsems[q][k % NSEM], 16 * (k // NSEM + 1))
                else:
                    ins = ent[1](e)
                    if ent[3] is not None:
                        q, k = ent[3]
                        if q == 'cc':
                            ins.then_inc(ccsem, 1)
                        else:
                            ins.then_inc(dsems[q][k % NSEM], 16)
                    elif ent[2]:
                        ins.then_inc(sems[engname], 1)

        with nc.Block() as block:
            @block.sync
            def _(e):
                run('sp', e)

            @block.scalar
            def _(e):
                run('act', e)

            @block.vector
            def _(e):
                run('dve', e)

            @block.gpsimd
            def _(e):
                run('pool', e)

            @block.tensor
            def _(e):
                run('pe', e)

    def mm(self, out, lhsT, rhs, start=True, stop=True):
        return self.op('pe', lambda e: e.matmul(out, lhsT, rhs, start=start, stop=stop),
                       ins=(lhsT, rhs), outs=(out,))

    def transpose(self, out, in_, ident):
        return self.op('pe', lambda e: e.transpose(out, in_, ident), ins=(in_, ident), outs=(out,))

    def act(self, out, in_, func, scale=1.0, bias=0.0, accum_out=None, eng='act'):
        outs = (out,) if accum_out is None else (out, accum_out)
        insl = [in_]
        if not isinstance(scale, (int, float)):
            insl.append(scale)
        if not isinstance(bias, (int, float)):
            insl.append(bias)
        if accum_out is None:
            f = lambda e: e.activation(out, in_, func, bias=bias, scale=scale)
        else:
            f = lambda e: e.activation(out, in_, func, bias=bias, scale=scale, accum_out=accum_out)
        return self.op('act', f, ins=insl, outs=outs)

    def copy(self, eng, out, in_):
        if eng == 'act':
            return self.op('act', lambda e: e.copy(out, in_), ins=(in_,), outs=(out,))
        return self.op(eng, lambda e: e.tensor_copy(out, in_), ins=(in_,), outs=(out,))

    def tt(self, eng, out, in0, in1, op):
        return self.op(eng, lambda e: e.tensor_tensor(out, in0, in1, op), ins=(in0, in1), outs=(out,))

    def ts(self, eng, out, in0, s1, op0, s2=None, op1=None):
        insl = [in0]
        if not isinstance(s1, (int, float)):
            insl.append(s1)
        if s2 is not None and not isinstance(s2, (int, float)):
            insl.append(s2)
        if op1 is None:
            f = lambda e: e.tensor_scalar(out, in0, s1, None, op0)
        else:
            f = lambda e: e.tensor_scalar(out, in0, s1, s2, op0, op1)
        return self.op(eng, f, ins=insl, outs=(out,))

    def stt(self, eng, out, in0, scalar, in1, op0, op1):
        insl = [in0, in1]
        if not isinstance(scalar, (int, float)):
            insl.append(scalar)
        return self.op(eng, lambda e: e.scalar_tensor_tensor(out, in0, scalar, in1, op0, op1),
                       ins=insl, outs=(out,))

    def recip(self, out, in_):
        return self.op('dve', lambda e: e.reciprocal(out, in_), ins=(in_,), outs=(out,))

    def memset(self, eng, ap, val):
        return self.op(eng, lambda e: e.memset(ap, val), ins=(), outs=(ap,))


def sap(t, off, pat, np_=128):
    base = t[:]
    p0 = base.ap[0]
    return bass.AP(t, off, [[p0[0], np_]] + [list(x) for x in pat])


EPS = 1e-6
PAD = 1024
DILS = (1, 4, 16)
SCALE = 128 ** -0.5
G = 512


class Cfg:
    def __init__(self, D=2048, FF=5632, T=8192, NSEG=2, DEPTH=4, HALO=False):
        self.D, self.FF, self.T, self.NSEG, self.DEPTH = D, FF, T, NSEG, DEPTH
        self.HALO = HALO
        self.NCH = D // 128
        self.NFF = FF // 128
        self.NG = T // G
        self.TP = T + 2 * PAD
        self.TSEG = T // NSEG


class Arena:
    def __init__(self, nc, base, limit, prefix):
        self.nc, self.off, self.limit, self.prefix = nc, base, limit, prefix

    def t(self, name, shape, dtype):
        nb = 4 if dtype == F32 else 2
        n = 1
        for s in shape[1:]:
            n *= s
        size = (n * nb + 63) // 64 * 64
        h = self.nc.alloc_sbuf_tensor_at(self.prefix + name, list(shape), dtype, offset=self.off)
        self.off += size
        assert self.off <= self.limit, (self.prefix + name, self.off, self.limit)
        return h


class Streamer:
    def __init__(self, P, ring, ns, blocks, q='sp', pf=None):
        self.P, self.ring, self.ns, self.blocks, self.q = P, ring, ns, blocks, q
        self.pf = pf if pf is not None else ns - 4
        self.issued = 0

    def slot(self, i):
        return i % self.ns

    def get(self, i):
        upto = min(len(self.blocks), i + self.pf + 1)
        while self.issued < upto:
            j = self.issued
            src = self.blocks[j]
            n = src.shape[-1]
            dst = self.ring[:, j % self.ns, 0:n]
            self.P.dma(self.q, dst, src)
            self.issued += 1
        return i % self.ns


def dap(t_ap, off, pat):
    return bass.AP(t_ap.tensor, off, [list(x) for x in pat])


def build(cfg):
    nc = bass.Bass("TRN2", target_bir_lowering=False)
    P = Prog(nc)
    D, FF, T, NSEG, L = cfg.D, cfg.FF, cfg.T, cfg.NSEG, cfg.DEPTH
    NCH, NFF, NG, TP, TSEG = cfg.NCH, cfg.NFF, cfg.NG, cfg.TP, cfg.TSEG
    KW = NCH * 128

    def din(name, shape, dt=F32):
        return nc.dram_tensor(name, list(shape), dt, kind="ExternalInput").ap()

    def dscr(name, shape, dt):
        return nc.dram_tensor(name, list(shape), dt, kind="Internal").ap()

    x_d = din("x", [T, D])
    mem_d = din("mem", [NSEG, 256, D])
    cos_d = din("cos", [128, T])
    sin_d = din("sin", [128, T])
    masks_d = din("masks", [128, 4, 256])
    ones_d = din("ones_f", [128, 128])
    rt_d = din("rt_f", [128, 128])
    ident_d = din("ident", [128, 128])
    gmix_d = din("g_mix", [128, L, NCH])
    gmem_d = din("g_mem", [128, L, NCH])
    gffn_d = din("g_ffn", [128, L, NCH])
    ggo_d = din("g_go", [128, L, 16])
    gfin_d = din("g_fin", [128, D])
    gsgu_d = din("g_sgu", [128, L, 512])
    bsp_d = din("b_sp", [128, L, 512])
    wsp_d = din("w_spT", [128, L, 4, 128])
    win_d = din("w_in_t", [L, 36, 128, KW])
    wout_d = din("w_out_t", [L, NCH, 128, 2048])
    wgu_d = din("w_gu_t", [L, 2 * NFF, 128, KW])
    wdn_d = din("w_dn_t", [L, NCH, 128, NFF * 128])
    wkv_d = din("w_kv_t", [L, 8, 128, KW])
    y_d = nc.dram_tensor("y", [T, D], F32, kind="ExternalOutput").ap()

    wb_in = dscr("wb_in", [L, 36, 128, KW], BF16)
    wb_out = dscr("wb_out", [L, NCH, 128, 2048], BF16)
    wb_gu = dscr("wb_gu", [L, 2 * NFF, 128, KW], BF16)
    wb_dn = dscr("wb_dn", [L, NCH, 128, NFF * 128], BF16)
    wb_kv = dscr("wb_kv", [L, 8, 128, KW], BF16)
    xT_s = dscr("xT_s", [NCH, 128, T], F32)
    KT_s = dscr("KT_s", [8, 128, TP], BF16)
    QT_s = dscr("QT_s", [8, 128, T], BF16)
    V_s = dscr("V_s", [8, TP, 128], BF16)
    bT_s = dscr("bT_s", [8, 128, T], BF16)
    mA_s = dscr("mA_s", [4, 128, T], BF16)
    mC_s = dscr("mC_s", [4, 128, T], BF16)
    if cfg.HALO:
        snd = [dscr("snd%d" % i, [1024, 1024], BF16) for i in range(4)]
        rcv = [dscr("rcv%d" % i, [2 * 1024, 1024], BF16) for i in range(4)]

    SB_LIMIT = 229344
    CA = Arena(nc, 16512, SB_LIMIT, "c_")
    ones_b = CA.t("ones", [128, 128], BF16)
    rt_b = CA.t("rt", [128, 128], BF16)
    ident = CA.t("ident", [128, 128], F32)
    masks_b = CA.t("masks", [128, 4, 256], BF16)
    gmix = CA.t("gmix", [128, L, NCH], F32)
    gmem = CA.t("gmem", [128, L, NCH], F32)
    gffn = CA.t("gffn", [128, L, NCH], F32)
    ggo = CA.t("ggo", [128, L, 16], F32)
    memKT = CA.t("memKT", [128, NSEG, 4, 256], BF16)
    memV = CA.t("memV", [128, NSEG, 2, 512], BF16)
    gsgu = CA.t("gsgu", [128, 512], F32)
    bsp = CA.t("bsp", [128, 512], F32)
    wspT = CA.t("wspT", [128, 4, 128], BF16)
    ctmp = CA.t("ctmp", [128, 1024], F32)
    zpad = CA.t("zpad", [128, 1024], BF16)
    ident_b = CA.t("ident_b", [128, 128], BF16)
    negm = CA.t("negm", [128, 4, 256], BF16)
    C0 = CA.off

    ps = [nc.alloc_psum_tensor("ps%d" % i, [128, 512], F32) for i in range(8)]

    def cast_list(l):
        out = []
        for (dst, src, n) in ((wb_kv, wkv_d, 8), (wb_in, win_d, 36), (wb_out, wout_d, NCH),
                              (wb_gu, wgu_d, 2 * NFF), (wb_dn, wdn_d, NCH)):
            step = 4
            for o in range(0, n, step):
                e = min(n, o + step)
                out.append((dst[l, o:e], src[l, o:e]))
        return out

    def cast_blocks(l):
        out = []
        for (dst, src, n, width) in ((wb_kv, wkv_d, 8, KW), (wb_in, win_d, 36, KW), (wb_out, wout_d, NCH, 2048),
                                     (wb_gu, wgu_d, 2 * NFF, KW), (wb_dn, wdn_d, NCH, NFF * 128)):
            for o in range(n):
                for c0 in range(0, width, 1024):
                    c1 = min(width, c0 + 1024)
                    out.append((dst[l, o, :, c0:c1], src[l, o, :, c0:c1]))
        return out

    def init():
        P.dma('sp', ctmp[:, 0:128], ones_d)
        P.copy('dve', ones_b[:], ctmp[:, 0:128])
        P.dma('sp', ctmp[:, 128:256], rt_d)
        P.copy('dve', rt_b[:], ctmp[:, 128:256])
        P.dma('sp', ident[:], ident_d)
        P.dma('sp', ctmp[:, 0:1024].rearrange("p (a b) -> p a b", a=4), masks_d)
        P.copy('dve', masks_b[:], ctmp[:, 0:1024].rearrange("p (a b) -> p a b", a=4))
        P.ts('dve', negm[:], ctmp[:, 0:1024].rearrange("p (a b) -> p a b", a=4), 30000.0, ALU.mult, -30000.0, ALU.add)
        P.copy('dve', ident_b[:], ident[:])
        P.dma('sp', gmix[:], gmix_d)
        P.dma('sp', gmem[:], gmem_d)
        P.dma('sp', gffn[:], gffn_d)
        P.dma('sp', ggo[:], ggo_d)
        for (dst, src) in cast_list(0):
            P.dma('pool', dst, src)
        z = zpad
        P.memset('dve', z[:], 0.0)
        for h in range(8):
            P.dma('sp', KT_s[h, :, 0:PAD], z[:, 0:PAD])
            P.dma('sp', KT_s[h, :, PAD + T:TP], z[:, 0:PAD])
            for b0 in (0, PAD + T):
                P.dma('sp', V_s[h, b0:b0 + PAD, :].rearrange("(a p) d -> p a d", p=128),
                      z[:, 0:1024].rearrange("p (a d) -> p a d", d=128))

    def norm_fm(src, ncs, N, gcol, dst, nfeat, sq, ps_ss, srt, rstd, eng_mul=('dve',)):
        for c in range(ncs):
            P.act(sq[:, c, 0:N], src(c), AF.Square)
        for c in range(ncs):
            P.mm(ps_ss[:, 0:N], ones_b[:], sq[:, c, 0:N], start=(c == 0), stop=(c == ncs - 1))
        P.act(srt[:, 0:N], ps_ss[:, 0:N], AF.Sqrt, scale=1.0 / nfeat, bias=EPS)
        P.recip(rstd[:, 0:N], srt[:, 0:N])
        for c in range(ncs):
            P.stt('dve', dst(c), src(c), gcol(c), rstd[:, 0:N], ALU.mult, ALU.mult)

    def pass0():
        A = Arena(nc, C0, SB_LIMIT, "p0_")
        xt = [A.t("x%d" % i, [128, D], F32) for i in range(2)]
        stg = [A.t("s%d" % i, [128, NCH, 128], F32) for i in range(2)]
        nb = 0
        for tt in range(T // 128):
            xs = xt[tt % 2]
            sg = stg[tt % 2]
            P.dma('sp', xs[:], x_d[tt * 128:(tt + 1) * 128, :])
            for c0 in range(0, NCH, 4):
                bank = ps[nb % 4]
                nb += 1
                nn = min(4, NCH - c0)
                for c in range(c0, c0 + nn):
                    P.transpose(bank[:, (c - c0) * 128:(c - c0 + 1) * 128], xs[:, c * 128:(c + 1) * 128], ident[:])
                P.copy('act' if (nb % 2) else 'dve', sg[:, c0:c0 + nn, :],
                       bank[:, 0:nn * 128].rearrange("p (a b) -> p a b", b=128))
            P.dma('act', xT_s[:, :, tt * 128:(tt + 1) * 128].rearrange("c p t -> p c t"), sg[:])

    def passM(l):
        A = Arena(nc, C0, SB_LIMIT, "pm_")
        mt_ = A.t("mem", [128, 2, D], F32)
        memT = A.t("memT", [128, NCH, 256], F32)
        sq = A.t("sq", [128, max(NCH, 8), 256], BF16)
        memn = A.t("memn", [128, NCH, 256], BF16)
        srt = A.t("srt", [128, 256], F32)
        rstd = A.t("rstd", [128, 256], F32)
        wkv = A.t("wkv", [128, 8, KW], BF16)
        tmpf = A.t("tmpf", [128, 512], F32)
        P.dma('sp', wkv[:], wb_kv[l].rearrange("o p n -> p o n"))
        P.dma('sp', gsgu[:], gsgu_d[:, l, :])
        P.dma('sp', bsp[:], bsp_d[:, l, :])
        P.dma('sp', tmpf[:].rearrange("p (a b) -> p a b", a=4), wsp_d[:, l])
        P.copy('dve', wspT[:], tmpf[:].rearrange("p (a b) -> p a b", a=4))
        nb = 0
        for s in range(NSEG):
            P.dma('sp', mt_[:], mem_d[s].rearrange("(a p) d -> p a d", p=128))
            for a in range(2):
                for c0 in range(0, NCH, 4):
                    bank = ps[nb % 2]
                    nb += 1
                    nn = min(4, NCH - c0)
                    for c in range(c0, c0 + nn):
                        P.transpose(bank[:, (c - c0) * 128:(c - c0 + 1) * 128], mt_[:, a, c * 128:(c + 1) * 128], ident[:])
                    P.copy('dve', memT[:, c0:c0 + nn, a * 128:(a + 1) * 128],
                           bank[:, 0:nn * 128].rearrange("p (a b) -> p a b", b=128))
            norm_fm(lambda c: memT[:, c, :], NCH, 256, lambda c: gmem[:, l, c:c + 1],
                    lambda c: memn[:, c, :], D, sq, ps[2], srt, rstd)
            for hc in range(4):
                bank = ps[3 + hc % 2]
                for kc in range(NCH):
                    P.mm(bank[:, 0:256], wkv[:, hc, kc * 128:(kc + 1) * 128], memn[:, kc, :],
                         start=(kc == 0), stop=(kc == NCH - 1))
                P.copy('act', memKT[:, s, hc, :], bank[:, 0:256])
            for m in range(2):
                bank = ps[5 + m % 2]
                for kc in range(NCH):
                    P.mm(bank[:], memn[:, kc, m * 128:(m + 1) * 128],
                         sap(wkv, 4 * KW + kc * 128, [[KW, 4], [1, 128]]),
                         start=(kc == 0), stop=(kc == NCH - 1))
                P.copy('act', memV[:, s, m, :], bank[:])

    def passA(l):
        A = Arena(nc, C0, SB_LIMIT, "pa_")
        xT = A.t("xT", [128, NCH, G], F32)
        sq = A.t("sq", [128, max(NCH, 8), G], BF16)
        hT = A.t("hT", [128, NCH, G], BF16)
        srt = A.t("srt", [128, G], F32)
        rstd = A.t("rstd", [128, G], F32)
        NS = 8
        ring = A.t("ring", [128, NS, KW], BF16)
        cs = [A.t("cos%d" % i, [128, G], F32) for i in range(2)]
        sn = [A.t("sin%d" % i, [128, G], F32) for i in range(2)]
        uT = A.t("uT", [128, 4, G], F32)
        vtm = A.t("vtm", [128, 4, 512], F32)
        vtmp = A.t("vtmp", [128, 512], F32)
        vv = A.t("vv", [128, 4, 512], BF16)
        ssq = A.t("ssq", [128, 16], F32)
        ssr = A.t("ssr", [128, 16], F32)
        junk = A.t("junk", [128, 128], BF16)
        aT = A.t("aT", [128, 4, G], F32)
        atmp = [A.t("atmp%d" % i, [128, G], F32) for i in range(2)]
        amix = A.t("amix", [128, 4, G], BF16)
        qcT = A.t("qcT", [128, 4, G], BF16)
        PT = [A.t("PT%d" % i, [128, G], BF16) for i in range(2)]
        rden = A.t("rden", [128, G], F32)
        cT = A.t("cT", [128, 4, G], F32)
        cmix = A.t("cmix", [128, 4, G], BF16)
        zb = [A.t("zb%d" % i, [128, G], BF16) for i in range(2)]
        t1 = [A.t("t1_%d" % i, [128, G], F32) for i in range(2)]
        t2 = [A.t("t2_%d" % i, [128, G], F32) for i in range(2)]
        ro = [A.t("ro%d" % i, [128, G], BF16) for i in range(2)]
        vout = [A.t("vout%d" % i, [128, 1024], BF16) for i in range(4)]

        order = list(range(0, 4)) + list(range(4, 8)) + list(range(32, 36)) + list(range(8, 32))
        blocks = []
        for g in range(NG):
            for oc in order:
                blocks.append(wb_in[l, oc])
        st = Streamer(P, ring, NS, blocks, 'sp')
        psA = [ps[0], ps[1], ps[2]]
        ps_ss = ps[3]
        psm = [ps[4], ps[5]]
        ps_num, ps_den = ps[6], ps[7]
        cnt = {'a': 0, 'm': 0, 'z': 0}

        def nextA():
            b = psA[cnt['a'] % 3]
            cnt['a'] += 1
            return b

        def nextM():
            b = psm[cnt['m'] % 2]
            cnt['m'] += 1
            return b

        for g in range(NG):
            t0 = g * G
            seg = t0 // TSEG
            bi = g * 36
            cg, sg_ = cs[g % 2], sn[g % 2]
            if g == 0:
                P.dma('pool', xT[:], xT_s[:, :, t0:t0 + G].rearrange("c p t -> p c t"))
                P.dma('pool', cg[:], cos_d[:, t0:t0 + G])
                P.dma('pool', sg_[:], sin_d[:, t0:t0 + G])
            norm_fm(lambda c: xT[:, c, :], NCH, G, lambda c: gmix[:, l, c:c + 1],
                    lambda c: hT[:, c, :], D, sq, ps_ss, srt, rstd, eng_mul=('dve', 'pool'))
            if g + 1 < NG:
                t1n = t0 + G
                P.dma('pool', xT[:], xT_s[:, :, t1n:t1n + G].rearrange("c p t -> p c t"))
                P.dma('pool', cs[(g + 1) % 2][:], cos_d[:, t1n:t1n + G])
                P.dma('pool', sn[(g + 1) % 2][:], sin_d[:, t1n:t1n + G])

            def proj_fm(bidx):
                s_ = st.get(bidx)
                bank = nextA()
                for kc in range(NCH):
                    P.mm(bank[:], ring[:, s_, kc * 128:(kc + 1) * 128], hT[:, kc, :],
                         start=(kc == 0), stop=(kc == NCH - 1))
                return bank

            def proj_tm(bidx0, tt):
                s0 = st.get(bidx0)
                for j in range(1, 4):
                    st.get(bidx0 + j)
                assert s0 % 4 == 0
                bank = nextA()
                for kc in range(NCH):
                    P.mm(bank[:], hT[:, kc, tt * 128:(tt + 1) * 128],
                         sap(ring, s0 * KW + kc * 128, [[KW, 4], [1, 128]]),
                         start=(kc == 0), stop=(kc == NCH - 1))
                return bank

            for i in range(4):
                bank = proj_fm(bi + i)
                P.act(uT[:, i, :], bank[:], AF.Gelu_apprx_tanh)
            P.memset('pool', ssq[:], 0.0)
            for tt in range(4):
                bank = proj_tm(bi + 4, tt)
                P.act(vtm[:, tt, :], bank[:], AF.Gelu_apprx_tanh)
                for gi in range(4):
                    P.act(junk[:], vtm[:, tt, gi * 128:(gi + 1) * 128], AF.Square,
                          accum_out=ssq[:, tt * 4 + gi:tt * 4 + gi + 1])
            P.act(ssr[:], ssq[:], AF.Sqrt, scale=1.0 / 128, bias=EPS)
            P.recip(ssq[:], ssr[:])
            for tt in range(4):
                P.tt('dve', vtmp[:].rearrange("p (a b) -> p a b", a=4),
                     vtm[:, tt, :].rearrange("p (a b) -> p a b", a=4),
                     sap(ssq, tt * 4, [[1, 4], [0, 128]]), ALU.mult)
                P.tt('pool', vv[:, tt, :], vtmp[:], gsgu[:], ALU.mult)
            for i in range(4):
                bank = proj_fm(bi + 8 + i)
                P.copy('act', qcT[:, i, :], bank[:])
            for gi in range(4):
                bank = nextM()
                for tt in range(4):
                    P.mm(bank[:, tt * 128:(tt + 1) * 128], vv[:, tt, gi * 128:(gi + 1) * 128], wspT[:, gi, :])
                at = atmp[gi % 2]
                P.tt('dve', at[:].rearrange("p (a b) -> p a b", a=4),
                     bank[:].rearrange("p (a b) -> p a b", a=4),
                     sap(bsp, gi * 128, [[0, 4], [1, 128]]), ALU.add)
                P.tt('pool', aT[:, gi, :], at[:], uT[:, gi, :], ALU.mult)
            norm_fm(lambda c: aT[:, c, :], 4, G, lambda c: ggo[:, l, c:c + 1],
                    lambda c: amix[:, c, :], 512, sq, ps_ss, srt, rstd, eng_mul=('pool',))
            P.dma('pool', mA_s[:, :, t0:t0 + G].rearrange("c p t -> p c t"), amix[:])
            def rope_a(i):
                bank = proj_fm(bi + 12 + i)
                k = cnt['z'] % 2
                cnt['z'] += 1
                P.copy('act', zb[k][:], bank[:])
                return (i, k)

            def rope_b(stt_):
                i, k = stt_
                h = i % 8
                isk = i >= 8
                z, o, a1, a2 = zb[k], ro[k], t1[k], t2[k]
                b2 = nextM()
                P.mm(b2[:], rt_b[:], z[:])
                P.tt('dve', a1[:], z[:], cg[:], ALU.mult)
                P.tt('dve', a2[:], b2[:], sg_[:], ALU.mult)
                P.tt('pool', o[:], a1[:], a2[:], ALU.add)
                if isk:
                    P.dma('pool', KT_s[h, :, PAD + t0:PAD + t0 + G], o[:])
                else:
                    P.dma('pool', QT_s[h, :, t0:t0 + G], o[:])

            def cross_a(hc):
                for m_ in range(2):
                    bank = nextM()
                    P.mm(bank[:], memKT[:, seg, hc, m_ * 128:(m_ + 1) * 128], qcT[:, hc, :])
                    P.act(PT[m_][:], bank[:], AF.Exp, scale=SCALE)

            def cross_b(hc):
                for m_ in range(2):
                    P.mm(ps_num[:], memV[:, seg, m_, hc * 128:(hc + 1) * 128], PT[m_][:], start=(m_ == 0), stop=(m_ == 1))
                for m_ in range(2):
                    P.mm(ps_den[:], ones_b[:], PT[m_][:], start=(m_ == 0), stop=(m_ == 1))
                P.recip(rden[:], ps_den[:])
                P.tt('dve', cT[:, hc, :], ps_num[:], rden[:], ALU.mult)

            pend = None
            for i in range(16):
                cur = rope_a(i)
                if pend is not None:
                    rope_b(pend)
                pend = cur
                if i in (1, 3, 5, 7):
                    cross_a((i - 1) // 2)
                if i in (2, 4, 6, 8):
                    cross_b((i - 2) // 2)
                if i == 10:
                    norm_fm(lambda c: cT[:, c, :], 4, G, lambda c: ggo[:, l, 12 + c:13 + c],
                            lambda c: cmix[:, c, :], 512, sq, ps_ss, srt, rstd, eng_mul=('pool',))
                    P.dma('pool', mC_s[:, :, t0:t0 + G].rearrange("c p t -> p c t"), cmix[:])
            rope_b(pend)
            for blk in range(2):
                for tt in range(4):
                    bank = proj_tm(bi + 28 + blk * 4, tt)
                    P.copy('act' if tt % 2 == 0 else 'dve', vout[tt][:, blk * 512:(blk + 1) * 512], bank[:])
            for tt in range(4):
                tk = PAD + t0 + tt * 128
                P.dma('pool', V_s[:, tk:tk + 128, :].rearrange("h p d -> p h d"),
                      vout[tt][:].rearrange("p (h d) -> p h d", h=8))

    def passB(l):
        A = Arena(nc, C0, SB_LIMIT, "pb_")
        KThs = [A.t("KTh%d" % i, [128, TP], BF16) for i in range(2)]
        QThs = [A.t("QTh%d" % i, [128, T], BF16) for i in range(2)]
        accns = [A.t("accn%d" % i, [128, T], F32) for i in range(2)]
        accds = [A.t("accd%d" % i, [128, T], F32) for i in range(2)]
        bTh = A.t("bTh", [128, T], BF16)
        NTV = 17
        NVS = 4
        vt = A.t("vt", [128, NVS, NTV * 128], BF16)
        psN = [ps[2], ps[3]]
        psD = [ps[4], ps[5]]
        cnt = {'s': 0, 'v': 0, 'p': 0}
        jobs = []
        for d in DILS:
            Lc = T // d
            nq = Lc // 128
            for r in range(d):
                i = 0
                while i <= nq:
                    n = min(NTV, nq + 1 - i)
                    jobs.append((d, r, i, n, nq))
                    i += n
        LA = 3
        NPT = 6
        PTt = [A.t("PTx%d" % i, [128, 256], BF16) for i in range(NPT)]
        psS = [ps[0], ps[1], ps[6], ps[7]]
        tiles = []
        for jn, (d, r, i0, n, nq) in enumerate(jobs):
            for ii in range(n):
                tiles.append((jn, d, r, i0 + ii, ii, nq))
        ntl = len(tiles)

        def vload(h, jn):
            d, r, i0, n, nq = jobs[jn]
            vs = jn % NVS
            off = (PAD + r + d * (128 * i0 - 64)) * 128
            src = dap(V_s[h], V_s[h].offset + off, [[d * 128, 128], [d * 128 * 128, n], [1, 128]])
            P.dma('sp', vt[:, vs, 0:n * 128].rearrange("p (a b) -> p a b", b=128), src)

        P.dma('sp', KThs[0][:], KT_s[0])
        P.dma('sp', QThs[0][:], QT_s[0])
        for h in range(8):
            KTh, QTh, accn, accd = KThs[h % 2], QThs[h % 2], accns[h % 2], accds[h % 2]
            if h + 1 < 8:
                P.dma('sp', KThs[(h + 1) % 2][:], KT_s[h + 1])
                P.dma('sp', QThs[(h + 1) % 2][:], QT_s[h + 1])
            vload(h, 0)
            for idx in range(ntl + LA):
                if idx < ntl:
                    jn, d, r, i, ii, nq = tiles[idx]
                    if ii == 0 and jn + 1 < len(jobs):
                        vload(h, jn + 1)
                    c0 = 128 if i == 0 else 0
                    c1 = 128 if i == nq else 256
                    kind = 0
                    if i == 0:
                        kind = 1
                    elif i == nq:
                        kind = 2
                    elif NSEG == 2 and i == nq // 2:
                        kind = 3
                    bS = psS[idx % 4]
                    pt = PTt[idx % NPT]
                    P.mm(bS[:, c0:c1], sap(KTh, PAD + r + d * (128 * i - 64), [[d, 128]]),
                         sap(QTh, r + d * (128 * (i - 1) + c0), [[d, c1 - c0]]), start=True, stop=False)
                    P.mm(bS[:, c0:c1], ident_b[:], negm[:, kind, c0:c1], start=False, stop=True)
                    P.act(pt[:, c0:c1], bS[:, c0:c1], AF.Exp, scale=SCALE)
                k2 = idx - LA
                if k2 >= 0:
                    jn, d, r, i, ii, nq = tiles[k2]
                    vs = jn % NVS
                    pt = PTt[k2 % NPT]
                    for hh in (0, 1):
                        j = i - 1 + hh
                        if j < 0 or j >= nq:
                            continue
                        P.mm(psN[j % 2][:, 0:128], vt[:, vs, ii * 128:(ii + 1) * 128], pt[:, hh * 128:(hh + 1) * 128],
                             start=(hh == 1), stop=(hh == 0))
                        P.mm(psD[j % 2][:, 0:128], ones_b[:], pt[:, hh * 128:(hh + 1) * 128],
                             start=(hh == 1), stop=(hh == 0))
                        if hh == 0:
                            tok = r + d * 128 * j
                            an = sap(accn, tok, [[d, 128]])
                            ad = sap(accd, tok, [[d, 128]])
                            if d == 1:
                                P.copy('dve', an, psN[j % 2][:, 0:128])
                                P.copy('act', ad, psD[j % 2][:, 0:128])
                            else:
                                P.tt('dve', an, psN[j % 2][:, 0:128], an, ALU.add)
                                P.tt('dve', ad, psD[j % 2][:, 0:128], ad, ALU.add)
            CH = 2048
            for c in range(0, T, CH):
                P.recip(accd[:, c:c + CH], accd[:, c:c + CH])
                P.tt('pool', bTh[:, c:c + CH], accn[:, c:c + CH], accd[:, c:c + CH], ALU.mult)
                P.dma('pool', bT_s[h, :, c:c + CH], bTh[:, c:c + CH])

    def passC(l):
        last = (l == L - 1)
        A = Arena(nc, C0, SB_LIMIT, "pc_")
        xT = A.t("xT", [128, NCH, G], F32)
        mixT = A.t("mixT", [128, 16, G], BF16)
        sq = A.t("sq", [128, max(NCH, 8), G], BF16)
        hT = A.t("hT", [128, NCH, G], BF16)
        srt = A.t("srt", [128, G], F32)
        rstd = A.t("rstd", [128, G], F32)
        actT = A.t("actT", [128, NFF, G], BF16)
        NS = 6 if last else 8
        ring = A.t("ring", [128, NS, 2048], BF16)
        sgt = [A.t("sg%d" % i, [128, G], F32) for i in range(2)]
        if last:
            xtm = [A.t("xtm%d" % i, [128, D], F32) for i in range(1)]
            ytm = [A.t("ytm%d" % i, [128, D], F32) for i in range(1)]
            gfin = A.t("gfin", [128, D], F32)
            ssq = A.t("ssq", [128, 8], F32)
            ssr = A.t("ssr", [128, 8], F32)
            P.dma('act', gfin[:], gfin_d)
        nparts = (NFF + 15) // 16
        blocks = []
        for g in range(NG):
            for oc in range(NCH):
                blocks.append(wb_out[l, oc])
            for j in range(NFF):
                blocks.append(wb_gu[l, j])
                blocks.append(wb_gu[l, NFF + j])
            for oc in range(NCH):
                for pt_ in range(nparts):
                    k0 = pt_ * 16
                    k1 = min(NFF, k0 + 16)
                    blocks.append(wb_dn[l, oc, :, k0 * 128:k1 * 128])
        st = Streamer(P, ring, NS, blocks, 'sp', pf=NS - 3)
        nblk = NCH + 2 * NFF + NCH * nparts
        psA = [ps[0], ps[1], ps[2], ps[3], ps[4], ps[5]]
        ps_ss = ps[6]
        psT = ps[7]
        cnt = {'a': 0, 'f': 0}

        def nextA():
            b = psA[cnt['a'] % 6]
            cnt['a'] += 1
            return b

        bgc = cast_blocks(l + 1) if l + 1 < L else []
        per = (len(bgc) + NG - 1) // NG if bgc else 0
        if bgc:
            stf = [A.t("stf%d" % i, [128, 1024], F32) for i in range(2)]
            stb = [A.t("stb%d" % i, [128, 1024], BF16) for i in range(2)]
        ccnt = {'k': 0}
        for g in range(NG):
            t0 = g * G
            bi = g * nblk
            blk = bgc[g * per:(g + 1) * per]
            for bi_, (dst_, src_) in enumerate(blk):
                k = ccnt['k']
                ccnt['k'] += 1
                n_ = src_.shape[-1]
                if bi_ == 0:
                    P.dma('pool', stf[k % 2][:, 0:n_], src_)
                if bi_ + 1 < len(blk):
                    nsrc = blk[bi_ + 1][1]
                    P.dma('pool', stf[(k + 1) % 2][:, 0:nsrc.shape[-1]], nsrc)
                P.copy('pool', stb[k % 2][:, 0:n_], stf[k % 2][:, 0:n_])
                P.dma('pool', dst_, stb[k % 2][:, 0:n_])
            P.dma('act', mixT[:, 4:12, :], bT_s[:, :, t0:t0 + G].rearrange("c p t -> p c t"))
            P.dma('act', mixT[:, 0:4, :], mA_s[:, :, t0:t0 + G].rearrange("c p t -> p c t"))
            P.dma('act', mixT[:, 12:16, :], mC_s[:, :, t0:t0 + G].rearrange("c p t -> p c t"))
            for c4 in range(0, NCH, 4):
                c5 = min(NCH, c4 + 4)
                P.dma('act', xT[:, c4:c5, :], xT_s[c4:c5, :, t0:t0 + G].rearrange("c p t -> p c t"))
            norm_fm(lambda c: mixT[:, 4 + c, :], 8, G, lambda c: ggo[:, l, 4 + c:5 + c],
                    lambda c: mixT[:, 4 + c, :], 1024, sq, ps_ss, srt, rstd, eng_mul=('dve', 'pool'))
            for oc in range(NCH):
                s_ = st.get(bi + oc)
                bank = nextA()
                for kc in range(16):
                    P.mm(bank[:], ring[:, s_, kc * 128:(kc + 1) * 128], mixT[:, kc, :],
                         start=(kc == 0), stop=(kc == 15))
                P.tt('dve', xT[:, oc, :], bank[:], xT[:, oc, :], ALU.add)
            norm_fm(lambda c: xT[:, c, :], NCH, G, lambda c: gffn[:, l, c:c + 1],
                    lambda c: hT[:, c, :], D, sq, ps_ss, srt, rstd, eng_mul=('dve', 'pool'))
            b0 = bi + NCH
            for j in range(NFF):
                sg_ = st.get(b0 + 2 * j)
                su_ = st.get(b0 + 2 * j + 1)
                bg = nextA()
                for kc in range(NCH):
                    P.mm(bg[:], ring[:, sg_, kc * 128:(kc + 1) * 128], hT[:, kc, :],
                         start=(kc == 0), stop=(kc == NCH - 1))
                bu = nextA()
                for kc in range(NCH):
                    P.mm(bu[:], ring[:, su_, kc * 128:(kc + 1) * 128], hT[:, kc, :],
                         start=(kc == 0), stop=(kc == NCH - 1))
                sgb = sgt[j % 2]
                P.act(sgb[:], bg[:], AF.Silu)
                P.tt('dve', actT[:, j, :], bu[:], sgb[:], ALU.mult)
            b1 = b0 + 2 * NFF
            for oc in range(NCH):
                bank = nextA()
                for pt_ in range(nparts):
                    s_ = st.get(b1 + oc * nparts + pt_)
                    k0 = pt_ * 16
                    k1 = min(NFF, k0 + 16)
                    for kc in range(k0, k1):
                        P.mm(bank[:], ring[:, s_, (kc - k0) * 128:(kc - k0 + 1) * 128], actT[:, kc, :],
                             start=(kc == 0), stop=(kc == NFF - 1))
                P.tt('dve', xT[:, oc, :], bank[:], xT[:, oc, :], ALU.add)
            if not last:
                P.dma('act', xT_s[:, :, t0:t0 + G].rearrange("c p t -> p c t"), xT[:])
            else:
                for tt in range(4):
                    xm = xtm[0]
                    ym = ytm[0]
                    for c0 in range(0, NCH, 4):
                        nn = min(4, NCH - c0)
                        for c in range(c0, c0 + nn):
                            P.transpose(psT[:, (c - c0) * 128:(c - c0 + 1) * 128], xT[:, c, tt * 128:(tt + 1) * 128], ident[:])
                        P.copy('act' if (c0 // 4) % 2 == 0 else 'dve', xm[:, c0 * 128:(c0 + nn) * 128], psT[:, 0:nn * 128])
                    P.memset('pool', ssq[:, 0:1], 0.0)
                    P.act(sap(sq, 0, [[1, D]]), xm[:], AF.Square, accum_out=ssq[:, 0:1])
                    P.act(ssr[:, 0:1], ssq[:, 0:1], AF.Sqrt, scale=1.0 / D, bias=EPS)
                    P.recip(ssq[:, 1:2], ssr[:, 0:1])
                    P.stt('dve', ym[:], xm[:], ssq[:, 1:2], gfin[:], ALU.mult, ALU.mult)
                    P.dma('act', y_d[t0 + tt * 128:t0 + (tt + 1) * 128, :], ym[:])

    def kview(buf, r0):
        return buf[r0:r0 + 1024, :].rearrange("(h p) t -> h p t", h=8)

    def vview(buf, r0):
        return buf[r0:r0 + 1024, :].rearrange("(h a) (b d) -> h (a b) d", h=8, b=8)

    def exchange(l):
        P.dma('sp', kview(snd[0], 0), KT_s[:, :, PAD:PAD + 1024])
        P.dma('sp', vview(snd[1], 0), V_s[:, PAD:PAD + 1024, :])
        P.dma('sp', kview(snd[2], 0), KT_s[:, :, PAD + T - 1024:PAD + T])
        P.dma('sp', vview(snd[3], 0), V_s[:, PAD + T - 1024:PAD + T, :])
        P.barrier()
        groups = [[0, 1], [2, 3], [4, 5], [6, 7]]
        for i in range(4):
            P.coll(lambda e, i=i: e.collective_compute(
                "AllGather", ALU.bypass, replica_groups=groups, ins=[snd[i]], outs=[rcv[i]]))
            P.barrier()
        P.dma('sp', KT_s[:, :, 0:PAD], kview(rcv[2], 0))
        P.dma('sp', V_s[:, 0:PAD, :], vview(rcv[3], 0))
        P.dma('sp', KT_s[:, :, PAD + T:TP], kview(rcv[0], 1024))
        P.dma('sp', V_s[:, PAD + T:TP, :], vview(rcv[1], 1024))

    init()
    pass0()
    P.barrier()
    for l in range(L):
        passM(l)
        P.barrier()
        passA(l)
        P.barrier()
        if cfg.HALO:
            exchange(l)
            P.barrier()
        passB(l)
        P.barrier()
        passC(l)
        P.barrier()
    P.emit()
    return nc


def rope_tables(pos):
    inv = (np.float32(500000.0) ** (-np.arange(0, 32, 2, dtype=np.float32) / np.float32(32))).astype(np.float32)
    ang = pos.astype(np.float32)[:, None] * inv[None, :]
    c = np.cos(ang).astype(np.float32).T
    s = np.sin(ang).astype(np.float32).T
    T = pos.shape[0]
    cos = np.ones((128, T), np.float32)
    sin = np.zeros((128, T), np.float32)
    cos[0:16] = c
    cos[16:32] = c
    sin[0:16] = s
    sin[16:32] = s
    return cos, sin


def make_masks(split_mid, left_nb=False, right_nb=False):
    kk = np.arange(128)[:, None]
    c = np.arange(256)[None, :]
    band = ((c >= kk) & (c <= kk + 128))
    m = np.zeros((128, 4, 256), np.float32)
    m[:, 0] = band
    m[:, 1] = band if left_nb else (band & (kk >= 64))
    m[:, 2] = band if right_nb else (band & (kk < 64))
    if split_mid:
        m[:, 3] = band & ((kk < 64) == (c < 128))
    else:
        m[:, 3] = band
    return m


def tile_w(w, kdim):
    Lw, K, N = w.shape
    a = w.reshape(Lw, K // 128, 128, N // 128, 128)
    return np.ascontiguousarray(a.transpose(0, 3, 2, 1, 4)).reshape(Lw, N // 128, 128, K)


def const_inputs(cfg, g_mix_norm, w_in, g_sgu, w_spatial, b_spatial, g_mem_norm, w_mem_kv,
                 g_group_out, w_out, g_ffn_norm, w_gate_up, w_down, g_final):
    L = cfg.DEPTH
    f = lambda a: np.ascontiguousarray(np.asarray(a, dtype=np.float32))

    def cols(g):
        g = f(g)
        return np.ascontiguousarray(g.reshape(L, -1, 128).transpose(2, 0, 1))
    rt = np.zeros((128, 128), np.float32)
    for i in range(16):
        rt[i + 16, i] = -1.0
        rt[i, i + 16] = 1.0
    d = {
        "ones_f": np.ones((128, 128), np.float32),
        "rt_f": rt,
        "ident": np.eye(128, dtype=np.float32),
        "g_mix": cols(g_mix_norm), "g_mem": cols(g_mem_norm), "g_ffn": cols(g_ffn_norm),
        "g_go": cols(g_group_out),
        "g_fin": np.ascontiguousarray(np.broadcast_to(f(g_final)[None, :], (128, cfg.D))),
        "g_sgu": np.ascontiguousarray(np.broadcast_to(f(g_sgu).reshape(L, 512)[None], (128, L, 512))),
        "b_sp": np.ascontiguousarray(np.broadcast_to(f(b_spatial).reshape(L, 512)[None], (128, L, 512))),
        "w_spT": np.ascontiguousarray(f(w_spatial).transpose(3, 0, 1, 2)),
        "w_in_t": tile_w(f(w_in), cfg.D),
        "w_out_t": tile_w(f(w_out), 2048),
        "w_gu_t": tile_w(f(w_gate_up), cfg.D),
        "w_dn_t": tile_w(f(w_down), cfg.FF),
        "w_kv_t": tile_w(f(w_mem_kv), cfg.D),
    }
    return d


SAMPLE_CORES = (2, 4, 5, 6)


def kernel(x_prompt, x_sample, mem_prompt, mem_sample, g_mix_norm, w_in, g_sgu, w_spatial,
           b_spatial, g_mem_norm, w_mem_kv, g_group_out, w_out, g_ffn_norm, w_gate_up,
           w_down, g_final):
    cfg = Cfg(D=2048, FF=5632, T=4096, NSEG=1, DEPTH=4, HALO=True)
    nc = build(cfg)
    ci = const_inputs(cfg, g_mix_norm, w_in, g_sgu, w_spatial, b_spatial, g_mem_norm, w_mem_kv,
                      g_group_out, w_out, g_ffn_norm, w_gate_up, w_down, g_final)
    f = lambda a: np.ascontiguousarray(np.asarray(a, dtype=np.float32))
    x_prompt, x_sample, mem_prompt, mem_sample = f(x_prompt), f(x_sample), f(mem_prompt), f(mem_sample)
    T, D = cfg.T, cfg.D
    cos0, sin0 = rope_tables(np.arange(T))
    cos1, sin1 = rope_tables(T + np.arange(T))
    zx = np.zeros((T, D), np.float32)
    zm = np.zeros((1, 256, D), np.float32)
    in_maps = []
    for c in range(8):
        d = dict(ci)
        if c == 0:
            d.update({"x": np.ascontiguousarray(x_prompt[0, 0:T]), "mem": mem_prompt,
                      "cos": cos0, "sin": sin0, "masks": make_masks(False, right_nb=True)})
        elif c == 1:
            d.update({"x": np.ascontiguousarray(x_prompt[0, T:2 * T]), "mem": mem_prompt,
                      "cos": cos1, "sin": sin1, "masks": make_masks(False, left_nb=True)})
        elif c in SAMPLE_CORES:
            s = SAMPLE_CORES.index(c)
            d.update({"x": np.ascontiguousarray(x_sample[s]), "mem": np.ascontiguousarray(mem_sample[s:s + 1]),
                      "cos": cos0, "sin": sin0, "masks": make_masks(False)})
        else:
            d.update({"x": zx, "mem": zm, "cos": cos0, "sin": sin0, "masks": make_masks(False)})
        in_maps.append(d)
    res = run_bass_kernel_spmd(nc, in_maps, core_ids=list(range(8)))
    r = res.results
    y_prompt = np.concatenate([np.asarray(r[0]["y"], dtype=np.float32),
                               np.asarray(r[1]["y"], dtype=np.float32)], axis=0).reshape(1, 2 * T, D)
    y_sample = np.stack([np.asarray(r[c]["y"], dtype=np.float32) for c in SAMPLE_CORES], axis=0)
    return (y_prompt, y_sample)
```
